# Optimizing a Trainium2 kernel written in Bass

```python
import math
import jax, jax.numpy as jnp
from jax import lax
import numpy as np

D_MODEL = 1024
BATCH = 8
SEQ = 4096
DEPTH = 1

PLE_DIM = 256
ROPE_THETA = 10000.0
RMS_EPS = 1e-6
Q_BLOCK = 128

D_MIX = D_MODEL
DIFF_WIDTH = D_MIX // 2
DIFF_HEADS = 4
DIFF_HD = DIFF_WIDTH // (2 * DIFF_HEADS)
MLA_WIDTH = D_MIX - DIFF_WIDTH
MLA_HEADS = 8
MLA_NOPE = 64
MLA_ROPE = 32
MLA_V = MLA_WIDTH // MLA_HEADS
MLA_Q_LORA = 384
MLA_KV_LORA = 128
SPLIT_SIZES = (DIFF_WIDTH, DIFF_WIDTH, DIFF_WIDTH, DIFF_WIDTH,
               MLA_Q_LORA, MLA_KV_LORA, MLA_ROPE, MLA_WIDTH)
D_IN = sum(SPLIT_SIZES)

kernel_name = "hymba_diffattn_mla_parallel_heads"


def rms_norm(x, g):
    xf = x.astype(jnp.float32)
    y = xf * lax.rsqrt(jnp.mean(xf * xf, axis=-1, keepdims=True) + RMS_EPS)
    return (y * g.astype(jnp.float32)).astype(x.dtype)


def rope(x, pos):
    d = x.shape[-1]
    inv = ROPE_THETA ** (-jnp.arange(0, d, 2, dtype=jnp.float32) / d)
    ang = pos.astype(jnp.float32)[..., None] * inv
    cos = jnp.cos(ang)[:, :, None, :]
    sin = jnp.sin(ang)[:, :, None, :]
    xf = x.astype(jnp.float32)
    x1, x2 = xf[..., : d // 2], xf[..., d // 2:]
    return jnp.concatenate([x1 * cos - x2 * sin, x2 * cos + x1 * sin], axis=-1).astype(x.dtype)


def to_blocks(t):
    b, s = t.shape[:2]
    return t.reshape(b, s // Q_BLOCK, Q_BLOCK, *t.shape[2:]).swapaxes(0, 1)


def from_blocks(t):
    nb, b, blk = t.shape[:3]
    return t.swapaxes(0, 1).reshape(b, nb * blk, *t.shape[3:])


def causal_probs(scores, start, seq):
    q_idx = start + jnp.arange(Q_BLOCK)
    k_idx = jnp.arange(seq)
    mask = k_idx[None, :] <= q_idx[:, None]
    return jax.nn.softmax(jnp.where(mask, scores, -jnp.inf), axis=-1)


def diff_attention(q, k, v, lam, pos):
    b, s, h, _, d = q.shape
    q = rope(q.reshape(b, s, 2 * h, d), pos).reshape(b, s, h, 2, d)
    k = rope(k.reshape(b, s, 2 * h, d), pos).reshape(b, s, h, 2, d)
    scale = d ** -0.5

    def block(args):
        qb, start = args
        sc = jnp.einsum('bqhmd,bkhmd->mbhqk', qb, k).astype(jnp.float32) * scale
        pr = causal_probs(sc, start, s)
        a = pr[0] - lam * pr[1]
        return jnp.einsum('bhqk,bkhe->bqhe', a.astype(v.dtype), v)

    starts = jnp.arange(s // Q_BLOCK) * Q_BLOCK
    return from_blocks(lax.map(block, (to_blocks(q), starts)))


def mla_attention(q_nope, q_rope, k_nope, k_rope, v):
    s = q_nope.shape[1]
    scale = (MLA_NOPE + MLA_ROPE) ** -0.5

    def block(args):
        qn, qr, start = args
        sc = (jnp.einsum('bqhd,bkhd->bhqk', qn, k_nope)
              + jnp.einsum('bqhr,bkr->bhqk', qr, k_rope)).astype(jnp.float32) * scale
        pr = causal_probs(sc, start, s)
        return jnp.einsum('bhqk,bkhd->bqhd', pr.astype(v.dtype), v)

    starts = jnp.arange(s // Q_BLOCK) * Q_BLOCK
    return from_blocks(lax.map(block, (to_blocks(q_nope), to_blocks(q_rope), starts)))


def setup_inputs(seed: int = 0) -> dict:
    key = jax.random.key(seed)
    ks = jax.random.split(key, 16)
    nrm = jax.random.normal
    f32 = jnp.float32
    x = nrm(ks[0], (BATCH, SEQ, D_MODEL), f32)
    p = nrm(ks[1], (DEPTH, BATCH, SEQ, PLE_DIM), f32)
    offset = jax.random.randint(ks[2], (BATCH, 1), 0, 1024, dtype=jnp.int32)
    positions = (offset + jnp.arange(SEQ, dtype=jnp.int32)[None, :]).astype(jnp.int32)
    norm_g = 1.0 + 0.02 * nrm(ks[3], (DEPTH, D_MODEL), f32)
    w_in = nrm(ks[4], (DEPTH, D_MODEL, D_IN), f32) * D_MODEL ** -0.5
    diff_lambda = 0.1 * nrm(ks[5], (DEPTH, 4, DIFF_HD), f32)
    diff_subln_g = 1.0 + 0.02 * nrm(ks[6], (DEPTH, 2 * DIFF_HD), f32)
    mla_q_norm_g = 1.0 + 0.02 * nrm(ks[7], (DEPTH, MLA_Q_LORA), f32)
    w_uq = nrm(ks[8], (DEPTH, MLA_Q_LORA, MLA_HEADS * (MLA_NOPE + MLA_ROPE)), f32) * MLA_Q_LORA ** -0.5
    mla_kv_norm_g = 1.0 + 0.02 * nrm(ks[9], (DEPTH, MLA_KV_LORA), f32)
    w_ukv = nrm(ks[10], (DEPTH, MLA_KV_LORA, MLA_HEADS * (MLA_NOPE + MLA_V)), f32) * MLA_KV_LORA ** -0.5
    w_out = nrm(ks[11], (DEPTH, D_MIX, D_MODEL), f32) * D_MIX ** -0.5
    w_ple = nrm(ks[12], (DEPTH, PLE_DIM, D_MODEL), f32) * PLE_DIM ** -0.5
    w_ple_gate = nrm(ks[13], (DEPTH, D_MODEL, D_MODEL), f32) * D_MODEL ** -0.5
    final_norm_g = 1.0 + 0.02 * nrm(ks[14], (D_MODEL,), f32)
    return {"x": x, "p": p, "positions": positions, "norm_g": norm_g, "w_in": w_in,
            "diff_lambda": diff_lambda, "diff_subln_g": diff_subln_g,
            "mla_q_norm_g": mla_q_norm_g, "w_uq": w_uq, "mla_kv_norm_g": mla_kv_norm_g,
            "w_ukv": w_ukv, "w_out": w_out, "w_ple": w_ple, "w_ple_gate": w_ple_gate,
            "final_norm_g": final_norm_g}


def reference(x, p, positions, norm_g, w_in, diff_lambda, diff_subln_g, mla_q_norm_g, w_uq,
              mla_kv_norm_g, w_ukv, w_out, w_ple, w_ple_gate, final_norm_g):
    b, s, _ = x.shape
    offsets = [int(o) for o in np.cumsum(SPLIT_SIZES)[:-1]]
    h = x
    for i in range(DEPTH):
        n = rms_norm(h, norm_g[i])
        proj = n @ w_in[i]
        dq, dk, dv, dgate, cq, ckv, kr, mgate = jnp.split(proj, offsets, axis=-1)

        lq1, lk1, lq2, lk2 = diff_lambda[i].astype(jnp.float32)
        lam_init = 0.8 - 0.6 * math.exp(-0.3 * i)
        lam = jnp.exp(jnp.sum(lq1 * lk1)) - jnp.exp(jnp.sum(lq2 * lk2)) + lam_init
        od = diff_attention(dq.reshape(b, s, DIFF_HEADS, 2, DIFF_HD),
                            dk.reshape(b, s, DIFF_HEADS, 2, DIFF_HD),
                            dv.reshape(b, s, DIFF_HEADS, 2 * DIFF_HD), lam, positions)
        od = rms_norm(od, diff_subln_g[i]) * (1.0 - lam_init)
        od = od.reshape(b, s, DIFF_WIDTH) * jax.nn.silu(dgate)

        q = (rms_norm(cq, mla_q_norm_g[i]) @ w_uq[i]).reshape(b, s, MLA_HEADS, MLA_NOPE + MLA_ROPE)
        q_nope = q[..., :MLA_NOPE]
        q_rope = rope(q[..., MLA_NOPE:], positions)
        kv = (rms_norm(ckv, mla_kv_norm_g[i]) @ w_ukv[i]).reshape(b, s, MLA_HEADS, MLA_NOPE + MLA_V)
        k_nope, v = kv[..., :MLA_NOPE], kv[..., MLA_NOPE:]
        k_rope = rope(kr[:, :, None, :], positions)[:, :, 0, :]
        om = mla_attention(q_nope, q_rope, k_nope, k_rope, v).reshape(b, s, MLA_WIDTH)
        om = om * jax.nn.silu(mgate)

        h = h + jnp.concatenate([od, om], axis=-1) @ w_out[i]

        h = h + (p[i] @ w_ple[i]) * jax.nn.sigmoid(h @ w_ple_gate[i])
    return rms_norm(h, final_norm_g)
```

```python
import contextlib
import numpy as np
import concourse.bass as bass
import concourse.mybir as mybir
from concourse.bass_utils import run_bass_kernel_spmd

F32 = mybir.dt.float32
BF16 = mybir.dt.bfloat16
I32 = mybir.dt.int32
AF = mybir.ActivationFunctionType
ALU = mybir.AluOpType
PI = float(np.pi)
EPS = 1e-6
ENGS = ("pe", "act", "dve", "pool", "sp")
SAME_ENGINE_SYNC = True
DEFER = 2


class Op:
    __slots__ = ("eng", "fn", "waits", "flag", "eidx", "chan", "count", "done_vc", "fcount")


class SemPool:
    def __init__(self, nc, stack, chans):
        self.sems = {e: stack.enter_context(nc.semaphore("sem_" + e)) for e in ENGS}
        self.csems = {c: stack.enter_context(nc.semaphore("c_" + c)) for c in chans}
        self.base = {e: 0 for e in ENGS}
        self.cbase = {c: 0 for c in chans}

    def all(self):
        return list(self.sems.values()) + list(self.csems.values())


class Sched:
    def __init__(self, pool):
        self.pool = pool
        self.streams = {e: [] for e in ENGS}
        self.last_w = {}
        self.readers = {}
        self.known = {e: {} for e in ENGS}
        self.chan_count = dict(pool.cbase)
        self.last_dma = {}

    def add(self, eng, fn, reads=(), writes=(), chan=None):
        op = Op()
        op.eng, op.fn, op.chan, op.flag = eng, fn, chan, False
        op.eidx = len(self.streams[eng])
        deps = []
        for b in reads:
            w = self.last_w.get(b)
            if w is not None:
                deps.append(w)
        for b in writes:
            w = self.last_w.get(b)
            if w is not None:
                deps.append(w)
            deps.extend(self.readers.get(b, ()))
        known = self.known[eng]
        waits = {}
        for d in deps:
            if d.chan is not None:
                key, val = ("chan", d.chan), self.chan_count[d.chan]
            else:
                key, val = d.eng, d.eidx
                if d.eng == eng and (eng == "pe" or not SAME_ENGINE_SYNC):
                    continue
            if known.get(key, -1) >= val:
                continue
            if key not in waits or waits[key][0] < val:
                waits[key] = (val, d)
        if chan is not None:
            prev = self.last_dma.get(chan)
            key = ("chan", chan)
            if prev is not None and known.get(key, -1) < self.chan_count[chan] and key not in waits:
                waits[key] = (self.chan_count[chan], prev)
        op.waits = [(d, v) for (v, d) in waits.values()]
        for d, _v in op.waits:
            if d.chan is None:
                d.flag = True
            for k, v in d.done_vc.items():
                if known.get(k, -1) < v:
                    known[k] = v
            if d.chan is not None:
                known[("chan", d.chan)] = _v
        if chan is not None:
            self.chan_count[chan] += 16
            op.count = self.chan_count[chan]
            op.done_vc = dict(known)
            op.done_vc[("chan", chan)] = op.count
            self.last_dma[chan] = op
        else:
            op.done_vc = dict(known)
            op.done_vc[eng] = op.eidx
        self.streams[eng].append(op)
        for b in writes:
            self.last_w[b] = op
            self.readers[b] = []
        for b in reads:
            if b not in writes:
                self.readers.setdefault(b, []).append(op)
        return op

    def emit(self, nc, final_chans=()):
        pool = self.pool
        for e in ENGS:
            c = pool.base[e]
            for op in self.streams[e]:
                if op.chan is None and op.flag:
                    c += 1
                op.fcount = c
            pool.base[e] = c
        with nc.Block() as block:
            decos = {"pe": block.tensor, "act": block.scalar, "dve": block.vector,
                     "pool": block.gpsimd, "sp": block.sync}
            for e in ENGS:
                def body(eo, e=e):
                    for op in self.streams[e]:
                        for d, v in op.waits:
                            if d.chan is not None:
                                eo.wait_ge(pool.csems[d.chan], v)
                            else:
                                eo.wait_ge(pool.sems[d.eng], d.fcount)
                        inst = op.fn(eo)
                        if op.chan is not None:
                            inst.then_inc(pool.csems[op.chan], 16)
                        elif op.flag:
                            inst.then_inc(pool.sems[e], 1)
                    if e == "sp":
                        for c in final_chans:
                            eo.wait_ge(pool.csems[c], self.chan_count[c])
                decos[e](body)
        pool.cbase = dict(self.chan_count)


CHANS = ["cst", "w0", "w1", "w2", "w3", "x0", "x1", "x2", "x3", "pos", "misc", "st0", "st1", "st2", "st3", "cat", "p0", "p1", "ct0", "ct1", "xf0", "xf1", "ct2", "xf2", "p2", "xf3"]


class Em:
    def __init__(self, S):
        self.S = S

    def mm(self, out, lhsT, rhs, start, stop, r, w, skip=False):
        self.S.add("pe", lambda e: e.matmul(out, lhsT=lhsT, rhs=rhs, start=start, stop=stop,
                                            skip_group_check=skip), r, w)

    def tr(self, out, in_, ident, r, w):
        self.S.add("pe", lambda e: e.transpose(out=out, in_=in_, identity=ident), r, w)

    def act(self, out, in_, func, r, w, scale=None, bias=None, accum=None):
        kw = {}
        if scale is not None:
            kw["scale"] = scale
        if bias is not None:
            kw["bias"] = bias
        if accum is not None:
            kw["accum_out"] = accum
        self.S.add("act", lambda e: e.activation(out=out, in_=in_, func=func, **kw), r, w)

    def ts(self, eng, out, in0, s1, op0, r, w, s2=None, op1=None):
        if op1 is None:
            self.S.add(eng, lambda e: e.tensor_scalar(out=out, in0=in0, scalar1=s1, scalar2=None, op0=op0), r, w)
        else:
            self.S.add(eng, lambda e: e.tensor_scalar(out=out, in0=in0, scalar1=s1, scalar2=s2, op0=op0, op1=op1), r, w)

    def tt(self, eng, out, in0, in1, op, r, w):
        self.S.add(eng, lambda e: e.tensor_tensor(out=out, in0=in0, in1=in1, op=op), r, w)

    def stt(self, out, in0, scalar, in1, op0, op1, r, w, accum=None):
        if accum is None:
            self.S.add("dve", lambda e: e.scalar_tensor_tensor(out=out, in0=in0, scalar=scalar, in1=in1,
                                                               op0=op0, op1=op1), r, w)
        else:
            self.S.add("dve", lambda e: e.scalar_tensor_tensor(out=out, in0=in0, scalar=scalar, in1=in1,
                                                               op0=op0, op1=op1, accum_out=accum), r, w)

    def cp(self, eng, out, in_, r, w):
        self.S.add(eng, lambda e: e.tensor_copy(out=out, in_=in_), r, w)

    def rcp(self, out, in_, r, w):
        self.S.add("dve", lambda e: e.reciprocal(out=out, in_=in_), r, w)

    def memset(self, eng, ap, val, w):
        self.S.add(eng, lambda e: e.memset(ap, val), (), w)

    def dma(self, out, in_, r, w, chan, q="sp"):
        self.S.add(q, lambda e: e.dma_start(out=out, in_=in_), r, w, chan=chan)


class Ctx:
    pass


def alloc_common(nc, st, G, pfx):
    def sb(name, shape, dt):
        return st.enter_context(nc.sbuf_tensor(pfx + name, shape, dt))
    G.sb = sb
    G.cstf = sb("cstf", [128, 482], F32)
    G.identb = sb("identb", [128, 128], BF16)
    G.trib = sb("trib", [128, 128], BF16)
    G.Pdb = sb("Pdb", [128, 128], BF16)
    G.Pmb = sb("Pmb", [128, 96], BF16)
    G.ngt = sb("ngt", [128, 8], F32)
    G.stage = [sb("stage%d" % i, [128, 512], F32) for i in range(4)]
    G.xin = [sb("xin%d" % i, [128, 1024], F32) for i in range(4)]
    G.xn = [sb("xn%d" % i, [128, 1024], BF16) for i in range(4)]
    G.xT = sb("xT", [128, 8, 512], BF16)
    G.stat = sb("stat", [128, 16], F32)
    G.ss = [st.enter_context(nc.psum_tensor(pfx + "ss%d" % i, [128, 1024], F32)) for i in range(2)]
    G.ps = [G.ss[0][:, 0:512], G.ss[0][:, 512:1024], G.ss[1][:, 0:512], G.ss[1][:, 512:1024]]
    G.ps += [st.enter_context(nc.psum_tensor(pfx + "ps%d" % i, [128, 512], F32))[:, :] for i in range(4, 8)]
    G.psT = G.ps[7].bitcast(BF16)
    G.junkb = [sb("junkb%d" % i, [128, 1024], BF16) for i in range(1)]
    G.mhalf = sb("mhalf", [128, 4], F32)
    G.jc = 0
    G.rr = 0
    G.nrot = 7
    G.qc = 0
    G.kc = 0


def load_consts(E, G, cst):
    E.dma(G.cstf[:], cst[:, :], [], ["cstf"], "cst")
    E.cp("dve", G.identb[:], G.cstf[:, 0:128], ["cstf"], ["identb"])
    E.ts("dve", G.trib[:], G.cstf[:, 128:256], -1.0, ALU.add, ["cstf"], ["trib"], s2=30000.0, op1=ALU.mult)
    E.cp("dve", G.Pdb[:], G.cstf[:, 256:384], ["cstf"], ["Pdb"])
    E.cp("dve", G.Pmb[:], G.cstf[:, 384:480], ["cstf"], ["Pmb"])


def load_weight(E, G, dst, src, ncols, gcol, key, cnt):
    c0 = 0
    while c0 < ncols:
        n = min(512, ncols - c0)
        s = cnt[0] % 4
        cnt[0] += 1
        E.dma(G.stage[s][:, 0:n], src[:, c0:c0 + n], [], [("stage", s)], "w%d" % s)
        if s % 2 == 0:
            if gcol is None:
                E.cp("dve", dst[:, c0:c0 + n], G.stage[s][:, 0:n], [("stage", s)], [key])
            else:
                E.ts("dve", dst[:, c0:c0 + n], G.stage[s][:, 0:n], gcol, ALU.mult, [("stage", s), "gcols"], [key])
        else:
            if gcol is None:
                E.act(dst[:, c0:c0 + n], G.stage[s][:, 0:n], AF.Copy, [("stage", s)], [key])
            else:
                E.act(dst[:, c0:c0 + n], G.stage[s][:, 0:n], AF.Copy, [("stage", s), "gcols"], [key], scale=gcol)
        c0 += n


def rstd(E, G, out, in_, scale, r, w):
    n = out.shape[1]
    E.ts("pool", out, in_, scale, ALU.mult, r, w, s2=EPS, op1=ALU.add)
    E.tt("pool", out, out, G.mhalf[:, 0:n], ALU.pow, list(w) + ["mhalf"], w)


def junk(G):
    i = 0
    return G.junkb[i], ("junk", i)


def next_ps(G):
    b = G.rr % G.nrot
    G.rr += 1
    return b


def x_load(E, G, x, t, j):
    E.dma(G.xin[j][:], x[t * 128:(t + 1) * 128, :], [], [("xin", j)], "x%d" % j)


def x_norm(E, G, j, rst_scale, on_dve=False):
    ssq = G.stat[:, j:j + 1]
    rs = G.stat[:, 4 + j:5 + j]
    if on_dve:
        E.stt(G.xn[j][:], G.xin[j][:], 1.0, G.xin[j][:], ALU.mult, ALU.mult, [("xin", j)], [("ssq", j), ("xn", j)],
              accum=ssq)
    else:
        jb, jk = junk(G)
        E.act(jb[:], G.xin[j][:], AF.Square, [("xin", j)], [("ssq", j), jk], accum=ssq)
    rstd(E, G, rs, ssq, rst_scale, [("ssq", j)], [("rs", j)])
    E.ts("dve", G.xn[j][:], G.xin[j][:], rs, ALU.mult, [("xin", j), ("rs", j)], [("xn", j)])


def x_transposes(E, G, j):
    for kc in range(8):
        E.tr(G.psT[:, kc * 128:(kc + 1) * 128], G.xn[j][:, kc * 128:(kc + 1) * 128], G.identb[:],
             [("xn", j), "identb"], [("ps", 7)])
    E.cp("dve", G.xT[:, :, j * 128:(j + 1) * 128],
         G.psT[:, :].rearrange("p (k t) -> p k t", k=8), [("ps", 7)], [("xT", j)])


def rope_tables_dve(E, G, pos, c, inv_col):
    E.dma(G.pI[:], pos[:, c * 512:(c + 1) * 512], [], ["pI"], "pos")
    for name, ub, sh in (("uS", G.uS, 0.0), ("uC", G.uC, 0.25)):
        E.ts("dve", ub[:], G.pI[:], inv_col, ALU.mult, ["pI", "cstf"], [name], s2=sh, op1=ALU.add)
        E.cp("dve", G.ni[:], ub[:], [name], ["ni"])
        E.cp("dve", G.nf[:], G.ni[:], ["ni"], ["nf"])
        E.tt("dve", ub[:], ub[:], G.nf[:], ALU.subtract, [name, "nf"], [name])
        E.stt(ub[:], ub[:], 0.5, ub[:], ALU.is_gt, ALU.subtract, [name], [name])


def rope_tables_act(E, G, c):
    sl = c % 2
    E.act(G.sinT[sl][:], G.uS[:], AF.Sin, ["uS"], [("sinT", sl)], scale=-2.0 * PI * (1.0 - 1e-6))
    E.act(G.cosT[sl][:], G.uC[:], AF.Sin, ["uC"], [("cosT", sl)], scale=-2.0 * PI * (1.0 - 1e-6))


def rope_apply(E, G, raw, rawkey, perm, permkey, nrow, dests, sl):
    b = next_ps(G)
    rawkeys = list(rawkey) if isinstance(rawkey, list) else [rawkey]
    E.mm(G.ps[b][0:nrow, :], perm, raw[0:nrow, :], True, True, rawkeys + [permkey], [("ps", b)])
    s = G.t1c % 2
    G.t1c += 1
    t1, t2 = G.t1[s], G.t2[s]
    E.tt("pool", t1[0:nrow, :], raw[0:nrow, :], G.cosT[sl][0:nrow, :], ALU.mult, rawkeys + [("cosT", sl)], [("t1", s)])
    E.tt("dve", t2[0:nrow, :], G.ps[b][0:nrow, :], G.sinT[sl][0:nrow, :], ALU.mult, [("ps", b), ("sinT", sl)], [("t2", s)])
    for (o, key, r0, r1) in dests:
        E.tt("dve", o, t1[r0:r1, :], t2[r0:r1, :], ALU.add, [("t1", s), ("t2", s)], [key])


def attention(E, G, c, units, scale, vw, pre_unit=None, hooks=None):
    nk = 4 * c + 4
    pairs = [(ui, r) for ui in range(len(units)) for r in range(nk // 2)]
    N = len(pairs)
    pend = []

    def diag_i(kt):
        return kt - 4 * c if kt >= 4 * c else 0

    spread = pre_unit is not None and c >= 1
    blocks = {}

    def qk(n):
        ui, r = pairs[n]
        U = units[ui]
        if r == 0 and pre_unit is not None:
            if spread:
                blocks[ui] = pre_unit(ui + 1)
            else:
                for blk in pre_unit(ui + 1):
                    blk()
        sbk, s = n % 2, n % 3
        qa = 128 * diag_i(2 * r)
        for t in range(2):
            kt = 2 * r + t
            q0 = 128 * diag_i(kt)
            b = 2 * sbk + t
            kap, kkey = U["KT"](kt)
            qap, qkey = U["QT"](q0)
            if kt < 4 * c:
                E.mm(G.ps[b][:, q0:512], kap, qap, True, True, list(kkey) + [qkey], [("ps", b)])
            else:
                E.mm(G.ps[b][:, q0:q0 + 128], kap, qap[:, 0:128], True, False, list(kkey) + [qkey], [("ps", b)])
                E.mm(G.ps[b][:, q0:q0 + 128], G.identb[:], G.trib[:], False, True, ["identb", "trib"], [("ps", b)])
                if q0 + 128 < 512:
                    E.mm(G.ps[b][:, q0 + 128:512], kap, qap[:, 128:512 - q0], True, True, list(kkey) + [qkey],
                         [("ps", b)])
        E.act(G.ET[s][:, :, qa:512], G.ss[sbk][:, :].rearrange("p (t q) -> p t q", t=2)[:, :, qa:512], AF.Exp,
              [("ps", 2 * sbk), ("ps", 2 * sbk + 1)], [("ET", s)], scale=scale)

    def pv(n):
        ui, r = pairs[n]
        U = units[ui]
        s = n % 3
        for t in range(2):
            kt = 2 * r + t
            i = diag_i(kt)
            vap, vkey = U["V"](kt)
            for j in range(i, 4):
                aap, akey, first = U["acc"](j)
                E.mm(aap, G.ET[s][:, t, 128 * j:128 * j + 128], vap, (kt == 0 and first), (kt == 4 * c + j),
                     [("ET", s), vkey], [akey], skip=True)
        if r == nk // 2 - 1:
            rest = U["epi"]()
            if rest is not None:
                pend.append((n + DEFER, rest))

    if pre_unit is not None:
        for blk in pre_unit(0):
            blk()
    qk(0)
    if N > 1:
        qk(1)
    for n in range(N):
        if n + 2 < N:
            qk(n + 2)
        pv(n)
        if spread:
            ui_, r_ = pairs[n]
            if blocks.get(ui_):
                blocks[ui_].pop(0)()
        if hooks and n in hooks:
            hooks.pop(n)()
        while pend and pend[0][0] <= n:
            pend.pop(0)[1]()
    while pend:
        pend.pop(0)[1]()
    assert all(not v for v in blocks.values())
    if hooks:
        for n in sorted(hooks):
            hooks[n]()


def build_nc(SL, dbg=False):
    NT = SL // 128
    NCH = SL // 512
    nc = bass.Bass("TRN2", target_bir_lowering=False)

    def din(name, shape, dt=F32):
        return nc.dram_tensor(name, shape, dt, kind="ExternalInput").ap()
    x = din("x", [SL, 1024])
    p = din("p", [SL, 256])
    pos = din("pos", [128, SL], I32)
    ng = din("ng", [128, 8])
    w_in = din("w_in", [1024, 3104])
    dl = din("dl", [128, 256])
    gsub = din("gsub", [128, 128])
    qg = din("qg", [128, 3])
    kvg = din("kvg", [128, 1])
    w_uq = din("w_uq", [384, 768])
    w_ukv = din("w_ukv", [128, 1024])
    w_out = din("w_out", [1024, 1024])
    w_ple = din("w_ple", [256, 1024])
    w_pg = din("w_pg", [1024, 1024])
    fg = din("fg", [128, 1024])
    cst = din("cst", [128, 482])
    out = nc.dram_tensor("out", [SL, 1024], F32, kind="ExternalOutput").ap()
    cat = nc.dram_tensor("cat", [SL, 1024], BF16, kind="Internal").ap()

    with contextlib.ExitStack() as gst:
        pool = SemPool(nc, gst, CHANS)
        with nc.Block() as blk0:
            @blk0.gpsimd
            def _(g):
                for sm in pool.all():
                    g.sem_clear(sm)

        with contextlib.ExitStack() as st:
            G = Ctx()
            alloc_common(nc, st, G, "A_")
            sb = G.sb
            S = Sched(pool)
            E = Em(S)
            Wd = sb("Wd", [128, 8, 2048], BF16)
            KT = sb("KTd", [128, 4, SL], BF16)
            V = sb("Vd", [128, NT, 4, 129], BF16)
            G.qraw = [sb("qraw%d" % i, [128, 512], BF16) for i in range(3)]
            QT0 = sb("QT0", [128, 4, 512], BF16)
            QT1 = sb("QT1", [128, 4, 512], BF16)
            G.pI = sb("pI", [128, 512], I32)
            G.ni = sb("ni", [128, 512], I32)
            G.uS = sb("uS", [128, 512], F32)
            G.uC = sb("uC", [128, 512], F32)
            G.nf = sb("nf", [128, 512], F32)
            G.cosT = [sb("cosT%d" % i, [128, 512], F32) for i in range(2)]
            G.sinT = [sb("sinT%d" % i, [128, 512], F32) for i in range(2)]
            G.t1 = [sb("t1_%d" % i, [128, 512], F32) for i in range(2)]
            G.t2 = [sb("t2_%d" % i, [128, 512], F32) for i in range(2)]
            G.t1c = 0
            gate = sb("gate", [128, 4, 512], BF16)
            G.ET = [sb("ET%d" % i, [128, 2, 512], BF16) for i in range(3)]
            tmp1 = [sb("tmp1_%d" % i, [128, 4, 128], F32) for i in range(2)]
            ob = [sb("ob_%d" % i, [128, 4, 128], F32) for i in range(2)]
            odt = [sb("odt%d" % i, [128, 4, 512], BF16) for i in range(2)]
            dlt = sb("dlt", [128, 256], F32)
            prod = sb("prod", [128, 128], F32)
            lamt = sb("lamt", [128, 8], F32)
            gs8 = sb("gs8", [128, 128], F32)
            est = sb("est", [128, 2, 24], F32)
            G.epsb = sb("epsb", [128, 1], F32)

            E.memset("pool", G.mhalf[:], -0.5, ["mhalf"])
            for h in range(4):
                E.memset("dve", QT0[:, h, :], 0.0, [("QT", h)])
                E.memset("dve", QT1[:, h, :], 0.0, [("QT", h)])
            load_consts(E, G, cst)
            E.dma(G.ngt[:], ng[:, :], [], ["gcols"], "misc")
            E.dma(dlt[:], dl[:, :], [], ["dlt"], "misc")
            E.dma(gs8[:], gsub[:, :], [], ["gs8"], "misc")
            E.ts("dve", gs8[:], gs8[:], 0.8, ALU.mult, ["gs8"], ["gs8"])
            E.tt("dve", prod[:, 0:64], dlt[:, 0:64], dlt[:, 64:128], ALU.mult, ["dlt"], ["prod"])
            E.tt("dve", prod[:, 64:128], dlt[:, 128:192], dlt[:, 192:256], ALU.mult, ["dlt", "prod"], ["prod"])
            jb, jk = junk(G)
            E.act(jb[:, 0:64], prod[:, 0:64], AF.Copy, ["prod"], ["lam0", jk], accum=lamt[:, 0:1])
            jb, jk = junk(G)
            E.act(jb[:, 0:64], prod[:, 64:128], AF.Copy, ["prod"], ["lam1", jk], accum=lamt[:, 1:2])
            E.act(lamt[:, 2:4], lamt[:, 0:2], AF.Exp, ["lam0", "lam1"], ["lam2"])
            E.tt("dve", lamt[:, 4:5], lamt[:, 2:3], lamt[:, 3:4], ALU.subtract, ["lam2"], ["lam4"])
            E.ts("dve", lamt[:, 5:6], lamt[:, 4:5], 0.2, ALU.add, ["lam4"], ["nlam"], s2=-1.0, op1=ALU.mult)
            nlam = lamt[:, 5:6]
            E.memset("dve", V[:, :, :, 128:129], 1.0, ["Vall"])
            for j in range(4):
                x_load(E, G, x, j, j)
            rope_tables_dve(E, G, pos, 0, G.cstf[:, 480:481])
            rope_tables_act(E, G, 0)
            if NCH > 1:
                rope_tables_dve(E, G, pos, 1, G.cstf[:, 480:481])
            for j in range(4):
                x_norm(E, G, j, 1.0 / 1024.0)
            for j in range(4):
                x_transposes(E, G, j)
            wcnt = [0]
            for cg in range(4):
                for kc in range(8):
                    load_weight(E, G, Wd[:, kc, cg * 512:(cg + 1) * 512],
                                w_in[kc * 128:(kc + 1) * 128, cg * 512:(cg + 1) * 512], 512,
                                G.ngt[:, kc:kc + 1], ("Wd", kc, cg), wcnt)
            xTk = [("xT", j) for j in range(4)]

            for c in range(NCH):
                sl = c % 2
                if c + 1 < NCH:
                    for j in range(4):
                        x_load(E, G, x, 4 * (c + 1) + j, j)
                pending = None
                for which in range(2):
                    for h in range(4):
                        b = next_ps(G)
                        col0 = which * 512 + h * 128
                        for kc in range(8):
                            E.mm(G.ps[b][:, :], Wd[:, kc, col0:col0 + 128], G.xT[:, kc, :], kc == 0, kc == 7,
                                 [("Wd", kc, which)] + xTk, [("ps", b)])
                        s = G.qc % 3
                        G.qc += 1
                        E.act(G.qraw[s][:], G.ps[b][:, :], AF.Copy, [("ps", b)], [("qraw", s)])
                        if which == 0:
                            dests = [(QT0[0:64, h, :], ("QT", h), 0, 64), (QT1[64:128, h, :], ("QT", h), 64, 128)]
                        else:
                            dests = [(KT[:, h, c * 512:(c + 1) * 512], ("KT", h, c), 0, 128)]
                        if pending is not None:
                            rope_apply(E, G, *pending)
                        pending = (G.qraw[s], ("qraw", s), G.Pdb[:], "Pdb", 128, dests, sl)
                if c + 1 < NCH and c < 3:
                    for j in range(4):
                        x_norm(E, G, j, 1.0 / 1024.0)
                for j in range(4):
                    t = 4 * c + j
                    b = next_ps(G)
                    for kc in range(8):
                        E.mm(G.ps[b][:, :], G.xT[:, kc, j * 128:(j + 1) * 128], Wd[:, kc, 1024:1536], kc == 0, kc == 7,
                             [("Wd", kc, 2), ("xT", j)], [("ps", b)])
                    E.cp("dve", V[:, t, :, 0:128], G.ps[b][:, :].rearrange("p (h e) -> p h e", h=4),
                         [("ps", b), "Vall"], [("V", t)])
                    if j == 0:
                        rope_apply(E, G, *pending)
                    b = next_ps(G)
                    for kc in range(8):
                        E.mm(G.ps[b][:, :], G.xT[:, kc, j * 128:(j + 1) * 128], Wd[:, kc, 1536:2048], kc == 0, kc == 7,
                             [("Wd", kc, 3), ("xT", j)], [("ps", b)])
                    E.act(gate[:, j, :], G.ps[b][:, :], AF.Silu, [("ps", b)], [("gate", j)])
                hooks = None
                if c + 1 < NCH:
                    rope_tables_act(E, G, c + 1)
                    if c == 0:
                        for j in range(4):
                            x_transposes(E, G, j)
                    elif c < 3:
                        hooks = {j: (lambda j=j: x_transposes(E, G, j)) for j in range(4)}
                    else:
                        hooks = {j: (lambda j=j: x_norm(E, G, j, 1.0 / 1024.0, on_dve=True)) for j in range(4)}
                        hooks.update({4 + j: (lambda j=j: x_transposes(E, G, j)) for j in range(4)})
                if c + 2 < NCH:
                    hooks = hooks if hooks is not None else {}
                    hooks[8] = (lambda c=c: rope_tables_dve(E, G, pos, c + 2, G.cstf[:, 480:481]))
                od_c = odt[c % 2]
                units = []
                for h in range(4):
                    for m in range(2):
                        ui = 2 * h + m
                        aset = ui % 2

                        def KTf(kt, h=h, m=m):
                            return KT[:, h, kt * 128:(kt + 1) * 128], [("KT", h, kt // 4)]

                        def QTf(q0, h=h, m=m):
                            return (QT0 if m == 0 else QT1)[:, h, q0:512], ("QT", h)

                        def Vf(kt, h=h):
                            return V[:, kt, h, :], ("V", kt)

                        def accf(j, aset=aset):
                            bank = 4 + 2 * aset + j // 2
                            o = (j % 2) * 129
                            return G.ps[bank][:, o:o + 129], ("ps", bank), (j % 2 == 0)

                        def epi(h=h, m=m, aset=aset, od_c=od_c):
                            hp = h % 2
                            eb = est[:, hp, :]
                            for j in range(4):
                                bank = 4 + 2 * aset + j // 2
                                o = (j % 2) * 129
                                acc = G.ps[bank]
                                rcol = eb[:, 4 * m + j:4 * m + j + 1]
                                E.rcp(rcol, acc[:, o + 128:o + 129], [("ps", bank)], [("r", hp, m, j)])
                                if m == 0:
                                    E.ts("dve", tmp1[hp][:, j, :], acc[:, o:o + 128], rcol, ALU.mult,
                                         [("ps", bank), ("r", hp, m, j)], [("tmp1", hp, j)])
                                else:
                                    rn = eb[:, 8 + j:9 + j]
                                    E.ts("dve", rn, rcol, nlam, ALU.mult, [("r", hp, m, j), "nlam"], [("rn", hp, j)])
                                    E.stt(ob[hp][:, j, :], acc[:, o:o + 128], rn, tmp1[hp][:, j, :], ALU.mult, ALU.add,
                                          [("ps", bank), ("rn", hp, j), ("tmp1", hp, j)], [("ob", hp, j)])
                            if m == 0:
                                return None

                            def rest():
                                for j in range(4):
                                    jb, jk = junk(G)
                                    E.act(jb[:, 0:128], ob[hp][:, j, :], AF.Square, [("ob", hp, j)],
                                          [("ss4", hp, j), jk], accum=eb[:, 12 + j:13 + j])
                                ss4k = [("ss4", hp, j) for j in range(4)]
                                rstd(E, G, eb[:, 20:24], eb[:, 12:16], 1.0 / 128.0, ss4k, [("rs4", hp)])
                                for j in range(4):
                                    E.stt(ob[hp][:, j, :], ob[hp][:, j, :], eb[:, 20 + j:21 + j], gs8[:], ALU.mult, ALU.mult,
                                          [("ob", hp, j), ("rs4", hp), "gs8"], [("ob", hp, j)])
                                E.tt("pool", od_c[:, :, h * 128:(h + 1) * 128], ob[hp][:, :, :],
                                     gate[:, :, h * 128:(h + 1) * 128], ALU.mult,
                                     [("ob", hp, j) for j in range(4)] + [("gate", j) for j in range(4)],
                                     [("odt", c % 2, h)])
                            return rest
                        units.append(dict(KT=KTf, QT=QTf, V=Vf, acc=accf, epi=epi))
                attention(E, G, c, units, 0.125, 128, hooks=hooks)
                E.dma(cat[c * 512:(c + 1) * 512, 0:512].rearrange("(j p) f -> p j f", p=128), od_c[:, :, :],
                      [("odt", c % 2, h) for h in range(4)], [], "cat", q="pool")
            S.emit(nc, final_chans=["cat"])

        with contextlib.ExitStack() as st:
            G = Ctx()
            alloc_common(nc, st, G, "B_")
            sb = G.sb
            S = Sched(pool)
            E = Em(S)
            Wm = sb("Wm", [128, 8, 1056], BF16)
            Wuq = sb("Wuq", [128, 3, 768], BF16)
            Wukv = sb("Wukv", [128, 1024], BF16)
            ckvT = sb("ckvT", [128, SL], BF16)
            Vm = sb("Vm", [128, NT, 8, 65], BF16)
            KTw = [sb("KTw%d" % i, [128, SL], BF16) for i in range(2)]
            G.qraw = [sb("qraw%d" % i, [128, 512], BF16) for i in range(3)]
            QT = sb("QTm", [128, 8, 512], BF16)
            G.pI = sb("pI", [128, 512], I32)
            G.ni = sb("ni", [128, 512], I32)
            G.uS = sb("uS", [128, 512], F32)
            G.uC = sb("uC", [128, 512], F32)
            G.nf = sb("nf", [128, 512], F32)
            G.cosT = [sb("cosT%d" % i, [128, 512], F32) for i in range(2)]
            G.sinT = [sb("sinT%d" % i, [128, 512], F32) for i in range(2)]
            G.t1 = [sb("t1_%d" % i, [128, 512], F32) for i in range(2)]
            G.t2 = [sb("t2_%d" % i, [128, 512], F32) for i in range(2)]
            G.t1c = 0
            gate = sb("gate", [128, 4, 512], BF16)
            G.ET = [sb("ET%d" % i, [128, 2, 512], BF16) for i in range(3)]
            omt = [sb("omt%d" % i, [128, 4, 512], BF16) for i in range(2)]
            cqn = [sb("cqn%d" % i, [128, 512], BF16) for i in range(4)]
            psT2 = [G.ps[6].bitcast(BF16), G.ps[7].bitcast(BF16)]
            G.nrot = 6
            nst = sb("nst", [128, 16], F32)
            cqT = sb("cqT", [128, 4, 512], BF16)
            krpad = [sb("krpad%d" % i, [128, 96], BF16) for i in range(4)]
            krraw = sb("krraw", [128, 512], BF16)
            gcol = sb("gcol", [128, 4], F32)
            est = sb("est", [128, 4, 4], F32)
            G.epsb = sb("epsb", [128, 1], F32)

            E.memset("pool", G.mhalf[:], -0.5, ["mhalf"])
            load_consts(E, G, cst)
            E.dma(G.ngt[:], ng[:, :], [], ["gcols"], "misc")
            E.dma(gcol[:, 0:3], qg[:, :], [], ["gcols"], "misc")
            E.dma(gcol[:, 3:4], kvg[:, :], [], ["gcols"], "misc")
            E.memset("dve", Vm[:, :, :, 64:65], 1.0, ["Vall"])
            for i in range(4):
                E.memset("pool", krpad[i][:], 0.0, ["krpad"])
            E.memset("pool", krraw[:], 0.0, ["krraw_all"])
            for j in range(4):
                x_load(E, G, x, j, j)
            rope_tables_dve(E, G, pos, 0, G.cstf[:, 481:482])
            rope_tables_act(E, G, 0)
            if NCH > 1:
                rope_tables_dve(E, G, pos, 1, G.cstf[:, 481:482])
            for j in range(4):
                x_norm(E, G, j, 1.0 / 1024.0)
            for j in range(4):
                x_transposes(E, G, j)
            wcnt = [0]
            for cg, (c0, c1) in enumerate(((0, 512), (512, 544), (544, 1056))):
                for kc in range(8):
                    load_weight(E, G, Wm[:, kc, c0:c1], w_in[kc * 128:(kc + 1) * 128, 2048 + c0:2048 + c1], c1 - c0,
                                G.ngt[:, kc:kc + 1], ("Wm", kc, cg), wcnt)
            for kc in range(3):
                load_weight(E, G, Wuq[:, kc, :], w_uq[kc * 128:(kc + 1) * 128, :], 768, gcol[:, kc:kc + 1], "Wuq", wcnt)
            load_weight(E, G, Wukv[:, :], w_ukv[:, :], 1024, gcol[:, 3:4], "Wukv", wcnt)
            xTk = [("xT", j) for j in range(4)]

            for c in range(NCH):
                sl = c % 2
                if c + 1 < NCH:
                    for j in range(4):
                        x_load(E, G, x, 4 * (c + 1) + j, j)
                for j in range(4):
                    b = next_ps(G)
                    for kc in range(8):
                        E.mm(G.ps[b][:, :], G.xT[:, kc, j * 128:(j + 1) * 128], Wm[:, kc, 0:512], kc == 0, kc == 7,
                             [("Wm", kc, 0), ("xT", j)], [("ps", b)])
                    sq, sk = nst[:, j:j + 1], nst[:, 4 + j:5 + j]
                    rq, rk = nst[:, 8 + j:9 + j], nst[:, 12 + j:13 + j]
                    jb, jk = junk(G)
                    E.act(jb[:, 0:384], G.ps[b][:, 0:384], AF.Square, [("ps", b)], [("sq", j), jk], accum=sq)
                    jb, jk = junk(G)
                    E.act(jb[:, 0:128], G.ps[b][:, 384:512], AF.Square, [("ps", b)], [("sk", j), jk], accum=sk)
                    rstd(E, G, rq, sq, 1.0 / 384.0, [("sq", j)], [("rq", j)])
                    rstd(E, G, rk, sk, 1.0 / 128.0, [("sk", j)], [("rk", j)])
                    E.ts("dve", cqn[j][:, 0:384], G.ps[b][:, 0:384], rq, ALU.mult, [("ps", b), ("rq", j)], [("cqn", j)])
                    E.ts("dve", cqn[j][:, 384:512], G.ps[b][:, 384:512], rk, ALU.mult, [("ps", b), ("rk", j)], [("cqn", j)])
                for j in range(4):
                    b = next_ps(G)
                    for kc in range(8):
                        E.mm(G.ps[b][:, 0:32], G.xT[:, kc, j * 128:(j + 1) * 128], Wm[:, kc, 512:544], kc == 0, kc == 7,
                             [("Wm", kc, 1), ("xT", j)], [("ps", b)])
                    E.cp("dve", krpad[j][:, 64:96], G.ps[b][:, 0:32], [("ps", b), "krpad"], [("krpad", j)])
                for j in range(4):
                    b = next_ps(G)
                    for kc in range(8):
                        E.mm(G.ps[b][:, :], G.xT[:, kc, j * 128:(j + 1) * 128], Wm[:, kc, 544:1056], kc == 0, kc == 7,
                             [("Wm", kc, 2), ("xT", j)], [("ps", b)])
                    E.act(gate[:, j, :], G.ps[b][:, :], AF.Silu, [("ps", b)], [("gate", j)])
                hooks = None
                if c + 1 < NCH:
                    rope_tables_act(E, G, c + 1)
                    hooks = {j: (lambda j=j: x_norm(E, G, j, 1.0 / 1024.0, on_dve=True)) for j in range(4)}
                    hooks.update({4 + j: (lambda j=j: x_transposes(E, G, j)) for j in range(4)})
                if c + 2 < NCH:
                    hooks[8] = (lambda c=c: rope_tables_dve(E, G, pos, c + 2, G.cstf[:, 481:482]))
                for j in range(4):
                    t = 4 * c + j
                    bk = 6 + j % 2
                    pT_ = psT2[j % 2]
                    for kc in range(4):
                        E.tr(pT_[:, kc * 128:(kc + 1) * 128], cqn[j][:, kc * 128:(kc + 1) * 128], G.identb[:],
                             [("cqn", j), "identb"], [("ps", bk)])
                    E.tr(pT_[0:96, 512:640], krpad[j][:, :], G.identb[:], [("krpad", j), "identb"], [("ps", bk)])
                    E.cp("dve", cqT[:, 0:3, j * 128:(j + 1) * 128],
                         pT_[:, 0:384].rearrange("p (k t) -> p k t", k=3), [("ps", bk)], [("cqT", j)])
                    E.cp("dve", ckvT[:, t * 128:(t + 1) * 128], pT_[:, 384:512], [("ps", bk)], [("ckvT", t)])
                    E.cp("dve", krraw[64:96, j * 128:(j + 1) * 128], pT_[64:96, 512:640], [("ps", bk), "krraw_all"],
                         [("krraw", j)])
                for j in range(4):
                    t = 4 * c + j
                    b = next_ps(G)
                    E.mm(G.ps[b][:, :], ckvT[:, t * 128:(t + 1) * 128], Wukv[:, 512:1024], True, True,
                         [("ckvT", t), "Wukv"], [("ps", b)])
                    E.cp("dve", Vm[:, t, :, 0:64], G.ps[b][:, :].rearrange("p (h e) -> p h e", h=8),
                         [("ps", b), "Vall"], [("V", t)])
                cqTk = [("cqT", j) for j in range(4)]
                pending = (krraw, [("krraw", j) for j in range(4)], G.Pmb[0:96, :], "Pmb", 96,
                           [(KTw[0][64:96, c * 512:(c + 1) * 512], ("KTr", 0, c), 64, 96),
                            (KTw[1][64:96, c * 512:(c + 1) * 512], ("KTr", 1, c), 64, 96)], sl)
                for h in range(8):
                    b = next_ps(G)
                    for kc in range(3):
                        E.mm(G.ps[b][0:96, :], Wuq[:, kc, h * 96:(h + 1) * 96], cqT[:, kc, :], kc == 0, kc == 2,
                             ["Wuq"] + cqTk, [("ps", b)])
                    s = G.qc % 3
                    G.qc += 1
                    E.act(G.qraw[s][0:96, :], G.ps[b][0:96, :], AF.Copy, [("ps", b)], [("qraw", s)])
                    rope_apply(E, G, *pending)
                    pending = (G.qraw[s], ("qraw", s), G.Pmb[0:96, :], "Pmb", 96,
                               [(QT[0:96, h, :], ("QT", h), 0, 96)], sl)
                rope_apply(E, G, *pending)
                om_c = omt[c % 2]

                def pre_unit(ui, c=c):
                    if ui >= 8:
                        return []
                    kb = ui % 2

                    def blk(jj):
                        b = 6 + (G.kc % 2)
                        G.kc += 1
                        E.mm(G.ps[b][:, :], Wukv[:, ui * 64:ui * 64 + 128], ckvT[:, jj * 512:(jj + 1) * 512], True, True,
                             ["Wukv"] + [("ckvT", 4 * jj + q) for q in range(4)], [("ps", b)])
                        E.cp("dve", KTw[kb][0:64, jj * 512:(jj + 1) * 512], G.ps[b][0:64, :], [("ps", b)],
                             [("KTn", kb, jj)])
                    return [(lambda jj=jj: blk(jj)) for jj in range(c + 1)]

                units = []
                for h in range(8):
                    kb = h % 2
                    bank = 4 + h % 2

                    def KTf(kt, kb=kb):
                        return KTw[kb][0:96, kt * 128:(kt + 1) * 128], [("KTn", kb, kt // 4), ("KTr", kb, kt // 4)]

                    def QTf(q0, h=h):
                        return QT[0:96, h, q0:512], ("QT", h)

                    def Vf(kt, h=h):
                        return Vm[:, kt, h, :], ("V", kt)

                    def accf(j, bank=bank):
                        return G.ps[bank][:, j * 65:j * 65 + 65], ("ps", bank), (j == 0)

                    def epi(h=h, bank=bank, om_c=om_c):
                        eb = est[:, h % 2, :]
                        for j in range(4):
                            acc = G.ps[bank]
                            rcol = eb[:, j:j + 1]
                            E.rcp(rcol, acc[:, j * 65 + 64:j * 65 + 65], [("ps", bank)], [("r", h % 2, j)])
                            E.stt(om_c[:, j, h * 64:(h + 1) * 64], acc[:, j * 65:j * 65 + 64], rcol,
                                  gate[:, j, h * 64:(h + 1) * 64], ALU.mult, ALU.mult,
                                  [("ps", bank), ("r", h % 2, j), ("gate", j)], [("omt", c % 2, h, j)])
                    units.append(dict(KT=KTf, QT=QTf, V=Vf, acc=accf, epi=epi))
                for h in range(8):
                    pass
                attention(E, G, c, units, float(96 ** -0.5), 64, pre_unit=pre_unit, hooks=hooks)
                E.dma(cat[c * 512:(c + 1) * 512, 512:1024].rearrange("(j p) f -> p j f", p=128), om_c[:, :, :],
                      [("omt", c % 2, h, j) for h in range(8) for j in range(4)], [], "cat", q="pool")
            S.emit(nc, final_chans=["cat"])

        with contextlib.ExitStack() as st:
            G = Ctx()
            alloc_common(nc, st, G, "C_")
            sb = G.sb
            S = Sched(pool)
            E = Em(S)
            Wout = sb("Wout", [128, 8, 1024], BF16)
            Wpg = sb("Wpg", [128, 8, 1024], BF16)
            Wple = sb("Wple", [128, 2, 1024], BF16)
            fgt = sb("fgt", [128, 1024], F32)
            ct = [sb("ct%d" % i, [128, 1024], BF16) for i in range(3)]
            xf = [sb("xf%d" % i, [128, 1024], F32) for i in range(4)]
            pin = [sb("pin%d" % i, [128, 256], F32) for i in range(3)]
            pb = [sb("pb%d" % i, [128, 256], BF16) for i in range(2)]
            pT = [sb("pT%d" % i, [128, 2, 128], BF16) for i in range(2)]
            catT = [sb("catT%d" % i, [128, 8, 128], BF16) for i in range(2)]
            hb = [sb("hb%d" % i, [128, 1024], BF16) for i in range(2)]
            hT = [sb("hT%d" % i, [128, 8, 128], BF16) for i in range(2)]
            sg = [sb("sg%d" % i, [128, 1024], F32) for i in range(2)]
            G.epsb = sb("epsb", [128, 1], F32)
            G.nrot = 6
            psTs = [(G.ps[6].bitcast(BF16), ("ps", 6)), (G.ps[7].bitcast(BF16), ("ps", 7))]
            tcnt = [0]

            def transposes(src, srckey, n, dst, dstkey, eng="dve"):
                pt, pk = psTs[tcnt[0] % 2]
                tcnt[0] += 1
                for kc in range(n):
                    E.tr(pt[:, kc * 128:(kc + 1) * 128], src[:, kc * 128:(kc + 1) * 128], G.identb[:],
                         [srckey, "identb"], [pk])
                E.cp(eng, dst[:, :, :], pt[:, 0:n * 128].rearrange("p (k t) -> p k t", k=n), [pk], [dstkey])

            E.memset("pool", G.mhalf[:], -0.5, ["mhalf"])
            load_consts(E, G, cst)
            E.dma(fgt[:], fg[:, :], [], ["fgt"], "misc")
            def loads(t):
                s3 = t % 3
                E.dma(ct[s3][:], cat[t * 128:(t + 1) * 128, :], [], [("ct", s3)], "ct%d" % s3)
                E.dma(xf[t % 4][:], x[t * 128:(t + 1) * 128, :], [], [("xf", t % 4)], "xf%d" % (t % 4))
                E.dma(pin[s3][:], p[t * 128:(t + 1) * 128, :], [], [("pin", s3)], "p%d" % s3)

            def T1(t):
                s, s3 = t % 2, t % 3
                transposes(ct[s3], ("ct", s3), 8, catT[s], ("catT", s))

            def M1(t):
                s, s4 = t % 2, t % 4
                for half in range(2):
                    hs = slice(half * 512, (half + 1) * 512)
                    b = next_ps(G)
                    for kc in range(8):
                        E.mm(G.ps[b][:, :], catT[s][:, kc, :], Wout[:, kc, hs], kc == 0, kc == 7,
                             [("catT", s), ("Wout", kc, half)], [("ps", b)])
                    E.tt("dve", xf[s4][:, hs], xf[s4][:, hs], G.ps[b][:, :], ALU.add, [("ps", b), ("xf", s4)],
                         [("xf", s4)])
                E.act(hb[s][:], xf[s4][:], AF.Copy, [("xf", s4)], [("hb", s)])
                E.cp("pool", pb[s][:], pin[t % 3][:], [("pin", t % 3)], [("pb", s)])

            def T23(t):
                s = t % 2
                transposes(hb[s], ("hb", s), 8, hT[s], ("hT", s))
                transposes(pb[s], ("pb", s), 2, pT[s], ("pT", s))

            def M23(t):
                s, s4 = t % 2, t % 4
                for half in range(2):
                    hs = slice(half * 512, (half + 1) * 512)
                    b = next_ps(G)
                    for kc in range(8):
                        E.mm(G.ps[b][:, :], hT[s][:, kc, :], Wpg[:, kc, hs], kc == 0, kc == 7,
                             [("hT", s), ("Wpg", kc, half)], [("ps", b)])
                    E.act(sg[s][:, hs], G.ps[b][:, :], AF.Sigmoid, [("ps", b)], [("sg", s, half)])
                    b = next_ps(G)
                    for kc in range(2):
                        E.mm(G.ps[b][:, :], pT[s][:, kc, :], Wple[:, kc, hs], kc == 0, kc == 1,
                             [("pT", s), ("Wple", kc, half)], [("ps", b)])
                    E.tt("dve", sg[s][:, hs], G.ps[b][:, :], sg[s][:, hs], ALU.mult, [("ps", b), ("sg", s, half)],
                         [("sg", s, half)])
                    E.tt("pool", xf[s4][:, hs], xf[s4][:, hs], sg[s][:, hs], ALU.add, [("sg", s, half), ("xf", s4)],
                         [("xf", s4)])

            def tail(t):
                s, s4 = t % 2, t % 4
                ssq = G.stat[:, s:s + 1]
                rs = G.stat[:, 2 + s:3 + s]
                jb, jk = junk(G)
                E.act(jb[:], xf[s4][:], AF.Square, [("xf", s4)], [("ssq", s), jk], accum=ssq)
                rstd(E, G, rs, ssq, 1.0 / 1024.0, [("ssq", s)], [("rs", s)])
                E.stt(xf[s4][:], xf[s4][:], rs, fgt[:], ALU.mult, ALU.mult, [("xf", s4), ("rs", s), "fgt"], [("xf", s4)])
                E.dma(out[t * 128:(t + 1) * 128, :], xf[s4][:], [("xf", s4)], [], "st%d" % s4, q="pool")

            loads(0)
            if NT > 1:
                loads(1)
            T1(0)
            wcnt = [0]
            for Wt, src, nk_, nm in ((Wout, w_out, 8, "Wout"), (Wpg, w_pg, 8, "Wpg"), (Wple, w_ple, 2, "Wple")):
                for half in range(2):
                    for kc in range(nk_):
                        load_weight(E, G, Wt[:, kc, half * 512:(half + 1) * 512],
                                    src[kc * 128:(kc + 1) * 128, half * 512:(half + 1) * 512], 512, None,
                                    (nm, kc, half), wcnt)

            M1(0)
            for t in range(NT):
                if t + 2 < NT:
                    loads(t + 2)
                if t + 1 < NT:
                    T1(t + 1)
                T23(t)
                if t >= 1:
                    tail(t - 1)
                if t + 1 < NT:
                    M1(t + 1)
                M23(t)
            tail(NT - 1)
            S.emit(nc, final_chans=["st0", "st1", "st2", "st3"])
    return nc


def make_consts():
    cstv = np.zeros((128, 482), np.float32)
    cstv[:, 0:128] = np.eye(128, dtype=np.float32)
    kk = np.arange(128)[:, None]
    qq = np.arange(128)[None, :]
    cstv[:, 128:256] = (kk <= qq).astype(np.float32)
    Pd = np.zeros((128, 128), np.float32)
    for m in range(128):
        if m % 64 < 32:
            Pd[m + 32, m] = -1.0
        else:
            Pd[m - 32, m] = 1.0
    cstv[:, 256:384] = Pd
    Pm = np.zeros((128, 96), np.float32)
    for m in range(64, 96):
        if (m - 64) < 16:
            Pm[m + 16, m] = -1.0
        else:
            Pm[m - 16, m] = 1.0
    cstv[:, 384:480] = Pm
    i32 = np.arange(128) % 32
    cstv[:, 480] = (10000.0 ** (-(2.0 * i32) / 64.0)) / (2.0 * np.pi)
    invm = np.zeros(128, np.float64)
    i16 = np.arange(32) % 16
    invm[64:96] = (10000.0 ** (-(2.0 * i16) / 32.0)) / (2.0 * np.pi)
    cstv[:, 481] = invm
    return cstv


_NC_CACHE = {}


def kernel(x, p, positions, norm_g, w_in, diff_lambda, diff_subln_g, mla_q_norm_g, w_uq,
           mla_kv_norm_g, w_ukv, w_out, w_ple, w_ple_gate, final_norm_g):
    x = np.asarray(x)
    B, SL, _ = x.shape
    f = lambda a: np.ascontiguousarray(np.asarray(a, dtype=np.float32))
    rep = lambda v, n=128: np.ascontiguousarray(np.broadcast_to(np.asarray(v, np.float32).reshape(1, -1), (n, np.asarray(v).size)))
    wukv = np.asarray(w_ukv[0], np.float32).reshape(128, 8, 2, 64)
    wukv_l = np.ascontiguousarray(np.concatenate([wukv[:, :, 0, :].reshape(128, 512), wukv[:, :, 1, :].reshape(128, 512)], axis=1))
    shared = {
        "ng": f(np.asarray(norm_g[0]).reshape(8, 128).T),
        "w_in": f(w_in[0]),
        "dl": rep(np.asarray(diff_lambda[0]).reshape(-1)),
        "gsub": rep(diff_subln_g[0]),
        "qg": f(np.asarray(mla_q_norm_g[0]).reshape(3, 128).T),
        "kvg": f(np.asarray(mla_kv_norm_g[0]).reshape(1, 128).T),
        "w_uq": f(w_uq[0]),
        "w_ukv": wukv_l,
        "w_out": f(w_out[0]),
        "w_ple": f(w_ple[0]),
        "w_pg": f(w_ple_gate[0]),
        "fg": rep(final_norm_g),
        "cst": make_consts(),
    }
    positions = np.asarray(positions)
    p = np.asarray(p)
    in_maps = []
    for b in range(B):
        m = dict(shared)
        m["x"] = f(x[b])
        m["p"] = f(p[0, b])
        m["pos"] = np.ascontiguousarray(np.broadcast_to(positions[b].astype(np.int32)[None, :], (128, SL)))
        in_maps.append(m)
    if SL not in _NC_CACHE:
        _NC_CACHE[SL] = build_nc(SL)
    nc = _NC_CACHE[SL]
    res = run_bass_kernel_spmd(nc, in_maps, core_ids=list(range(B)))
    return np.stack([np.asarray(r["out"], dtype=np.float32) for r in res.results], axis=0)
```

```python
import contextlib
import numpy as np
import concourse.bass as bass
import concourse.mybir as mybir
from concourse.bass_utils import run_bass_kernel_spmd

F32 = mybir.dt.float32
BF16 = mybir.dt.bfloat16
I32 = mybir.dt.int32
AF = mybir.ActivationFunctionType
ALU = mybir.AluOpType
PI = float(np.pi)
EPS = 1e-6
ENGS = ("pe", "act", "dve", "pool", "sp")
SAME_ENGINE_SYNC = True
DEFER = 2


class Op:
    __slots__ = ("eng", "fn", "waits", "flag", "eidx", "chan", "count", "done_vc", "fcount")


class SemPool:
    def __init__(self, nc, stack, chans):
        self.sems = {e: stack.enter_context(nc.semaphore("sem_" + e)) for e in ENGS}
        self.csems = {c: stack.enter_context(nc.semaphore("c_" + c)) for c in chans}
        self.base = {e: 0 for e in ENGS}
        self.cbase = {c: 0 for c in chans}

    def all(self):
        return list(self.sems.values()) + list(self.csems.values())


class Sched:
    def __init__(self, pool):
        self.pool = pool
        self.streams = {e: [] for e in ENGS}
        self.last_w = {}
        self.readers = {}
        self.known = {e: {} for e in ENGS}
        self.chan_count = dict(pool.cbase)
        self.last_dma = {}

    def add(self, eng, fn, reads=(), writes=(), chan=None):
        op = Op()
        op.eng, op.fn, op.chan, op.flag = eng, fn, chan, False
        op.eidx = len(self.streams[eng])
        deps = []
        for b in reads:
            w = self.last_w.get(b)
            if w is not None:
                deps.append(w)
        for b in writes:
            w = self.last_w.get(b)
            if w is not None:
                deps.append(w)
            deps.extend(self.readers.get(b, ()))
        known = self.known[eng]
        waits = {}
        for d in deps:
            if d.chan is not None:
                key, val = ("chan", d.chan), self.chan_count[d.chan]
            else:
                key, val = d.eng, d.eidx
                if d.eng == eng and (eng == "pe" or not SAME_ENGINE_SYNC):
                    continue
            if known.get(key, -1) >= val:
                continue
            if key not in waits or waits[key][0] < val:
                waits[key] = (val, d)
        if chan is not None:
            prev = self.last_dma.get(chan)
            key = ("chan", chan)
            if prev is not None and known.get(key, -1) < self.chan_count[chan] and key not in waits:
                waits[key] = (self.chan_count[chan], prev)
        op.waits = [(d, v) for (v, d) in waits.values()]
        for d, _v in op.waits:
            if d.chan is None:
                d.flag = True
            for k, v in d.done_vc.items():
                if known.get(k, -1) < v:
                    known[k] = v
            if d.chan is not None:
                known[("chan", d.chan)] = _v
        if chan is not None:
            self.chan_count[chan] += 16
            op.count = self.chan_count[chan]
            op.done_vc = dict(known)
            op.done_vc[("chan", chan)] = op.count
            self.last_dma[chan] = op
        else:
            op.done_vc = dict(known)
            op.done_vc[eng] = op.eidx
        self.streams[eng].append(op)
        for b in writes:
            self.last_w[b] = op
            self.readers[b] = []
        for b in reads:
            if b not in writes:
                self.readers.setdefault(b, []).append(op)
        return op

    def emit(self, nc, final_chans=()):
        pool = self.pool
        for e in ENGS:
            c = pool.base[e]
            for op in self.streams[e]:
                if op.chan is None and op.flag:
                    c += 1
                op.fcount = c
            pool.base[e] = c
        with nc.Block() as block:
            decos = {"pe": block.tensor, "act": block.scalar, "dve": block.vector,
                     "pool": block.gpsimd, "sp": block.sync}
            for e in ENGS:
                def body(eo, e=e):
                    for op in self.streams[e]:
                        for d, v in op.waits:
                            if d.chan is not None:
                                eo.wait_ge(pool.csems[d.chan], v)
                            else:
                                eo.wait_ge(pool.sems[d.eng], d.fcount)
                        inst = op.fn(eo)
                        if op.chan is not None:
                            inst.then_inc(pool.csems[op.chan], 16)
                        elif op.flag:
                            inst.then_inc(pool.sems[e], 1)
                    if e == "sp":
                        for c in final_chans:
                            eo.wait_ge(pool.csems[c], self.chan_count[c])
                decos[e](body)
        pool.cbase = dict(self.chan_count)


CHANS = ["cst", "w0", "w1", "w2", "w3", "x0", "x1", "x2", "x3", "pos", "misc", "st0", "st1", "st2", "st3", "cat", "p0", "p1", "ct0", "ct1", "xf0", "xf1", "ct2", "xf2", "p2", "xf3"]


class Em:
    def __init__(self, S):
        self.S = S

    def mm(self, out, lhsT, rhs, start, stop, r, w, skip=False):
        self.S.add("pe", lambda e: e.matmul(out, lhsT=lhsT, rhs=rhs, start=start, stop=stop,
                                            skip_group_check=skip), r, w)

    def tr(self, out, in_, ident, r, w):
        self.S.add("pe", lambda e: e.transpose(out=out, in_=in_, identity=ident), r, w)

    def act(self, out, in_, func, r, w, scale=None, bias=None, accum=None):
        kw = {}
        if scale is not None:
            kw["scale"] = scale
        if bias is not None:
            kw["bias"] = bias
        if accum is not None:
            kw["accum_out"] = accum
        self.S.add("act", lambda e: e.activation(out=out, in_=in_, func=func, **kw), r, w)

    def ts(self, eng, out, in0, s1, op0, r, w, s2=None, op1=None):
        if op1 is None:
            self.S.add(eng, lambda e: e.tensor_scalar(out=out, in0=in0, scalar1=s1, scalar2=None, op0=op0), r, w)
        else:
            self.S.add(eng, lambda e: e.tensor_scalar(out=out, in0=in0, scalar1=s1, scalar2=s2, op0=op0, op1=op1), r, w)

    def tt(self, eng, out, in0, in1, op, r, w):
        self.S.add(eng, lambda e: e.tensor_tensor(out=out, in0=in0, in1=in1, op=op), r, w)

    def stt(self, out, in0, scalar, in1, op0, op1, r, w):
        self.S.add("dve", lambda e: e.scalar_tensor_tensor(out=out, in0=in0, scalar=scalar, in1=in1,
                                                           op0=op0, op1=op1), r, w)

    def cp(self, eng, out, in_, r, w):
        self.S.add(eng, lambda e: e.tensor_copy(out=out, in_=in_), r, w)

    def rcp(self, out, in_, r, w):
        self.S.add("dve", lambda e: e.reciprocal(out=out, in_=in_), r, w)

    def memset(self, eng, ap, val, w):
        self.S.add(eng, lambda e: e.memset(ap, val), (), w)

    def dma(self, out, in_, r, w, chan, q="sp"):
        self.S.add(q, lambda e: e.dma_start(out=out, in_=in_), r, w, chan=chan)


class Ctx:
    pass


def alloc_common(nc, st, G, pfx):
    def sb(name, shape, dt):
        return st.enter_context(nc.sbuf_tensor(pfx + name, shape, dt))
    G.sb = sb
    G.cstf = sb("cstf", [128, 482], F32)
    G.identb = sb("identb", [128, 128], BF16)
    G.trib = sb("trib", [128, 128], BF16)
    G.Pdb = sb("Pdb", [128, 128], BF16)
    G.Pmb = sb("Pmb", [128, 96], BF16)
    G.ngt = sb("ngt", [128, 8], F32)
    G.stage = [sb("stage%d" % i, [128, 512], F32) for i in range(4)]
    G.xin = [sb("xin%d" % i, [128, 1024], F32) for i in range(4)]
    G.xn = [sb("xn%d" % i, [128, 1024], BF16) for i in range(4)]
    G.xT = sb("xT", [128, 8, 512], BF16)
    G.stat = sb("stat", [128, 16], F32)
    G.ss = [st.enter_context(nc.psum_tensor(pfx + "ss%d" % i, [128, 1024], F32)) for i in range(2)]
    G.ps = [G.ss[0][:, 0:512], G.ss[0][:, 512:1024], G.ss[1][:, 0:512], G.ss[1][:, 512:1024]]
    G.ps += [st.enter_context(nc.psum_tensor(pfx + "ps%d" % i, [128, 512], F32))[:, :] for i in range(4, 8)]
    G.psT = G.ps[7].bitcast(BF16)
    G.junkb = [sb("junkb%d" % i, [128, 1024], BF16) for i in range(1)]
    G.mhalf = sb("mhalf", [128, 4], F32)
    G.jc = 0
    G.rr = 0
    G.nrot = 7
    G.qc = 0
    G.kc = 0


def load_consts(E, G, cst):
    E.dma(G.cstf[:], cst[:, :], [], ["cstf"], "cst")
    E.cp("dve", G.identb[:], G.cstf[:, 0:128], ["cstf"], ["identb"])
    E.ts("dve", G.trib[:], G.cstf[:, 128:256], -1.0, ALU.add, ["cstf"], ["trib"], s2=30000.0, op1=ALU.mult)
    E.cp("dve", G.Pdb[:], G.cstf[:, 256:384], ["cstf"], ["Pdb"])
    E.cp("dve", G.Pmb[:], G.cstf[:, 384:480], ["cstf"], ["Pmb"])


def load_weight(E, G, dst, src, ncols, gcol, key, cnt):
    c0 = 0
    while c0 < ncols:
        n = min(512, ncols - c0)
        s = cnt[0] % 4
        cnt[0] += 1
        E.dma(G.stage[s][:, 0:n], src[:, c0:c0 + n], [], [("stage", s)], "w%d" % s)
        if s % 2 == 0:
            if gcol is None:
                E.cp("dve", dst[:, c0:c0 + n], G.stage[s][:, 0:n], [("stage", s)], [key])
            else:
                E.ts("dve", dst[:, c0:c0 + n], G.stage[s][:, 0:n], gcol, ALU.mult, [("stage", s), "gcols"], [key])
        else:
            if gcol is None:
                E.act(dst[:, c0:c0 + n], G.stage[s][:, 0:n], AF.Copy, [("stage", s)], [key])
            else:
                E.act(dst[:, c0:c0 + n], G.stage[s][:, 0:n], AF.Copy, [("stage", s), "gcols"], [key], scale=gcol)
        c0 += n


def rstd(E, G, out, in_, scale, r, w):
    n = out.shape[1]
    E.ts("pool", out, in_, scale, ALU.mult, r, w, s2=EPS, op1=ALU.add)
    E.tt("pool", out, out, G.mhalf[:, 0:n], ALU.pow, list(w) + ["mhalf"], w)


def junk(G):
    i = 0
    return G.junkb[i], ("junk", i)


def next_ps(G):
    b = G.rr % G.nrot
    G.rr += 1
    return b


def x_load(E, G, x, t, j):
    E.dma(G.xin[j][:], x[t * 128:(t + 1) * 128, :], [], [("xin", j)], "x%d" % j)


def x_norm(E, G, j, rst_scale, part="all"):
    ssq = G.stat[:, j:j + 1]
    rs = G.stat[:, 4 + j:5 + j]
    if part in ("all", "sq"):
        jb, jk = junk(G)
        E.act(jb[:], G.xin[j][:], AF.Square, [("xin", j)], [("ssq", j), jk], accum=ssq)
    if part in ("all", "cast"):
        rstd(E, G, rs, ssq, rst_scale, [("ssq", j)], [("rs", j)])
        E.ts("dve", G.xn[j][:], G.xin[j][:], rs, ALU.mult, [("xin", j), ("rs", j)], [("xn", j)])


def x_transposes(E, G, j):
    for kc in range(8):
        E.tr(G.psT[:, kc * 128:(kc + 1) * 128], G.xn[j][:, kc * 128:(kc + 1) * 128], G.identb[:],
             [("xn", j), "identb"], [("ps", 7)])
    E.cp("dve", G.xT[:, :, j * 128:(j + 1) * 128],
         G.psT[:, :].rearrange("p (k t) -> p k t", k=8), [("ps", 7)], [("xT", j)])


def rope_tables_dve(E, G, pos, c, inv_col):
    E.dma(G.pI[:], pos[:, c * 512:(c + 1) * 512], [], ["pI"], "pos")
    for name, ub, sh in (("uS", G.uS, 0.0), ("uC", G.uC, 0.25)):
        E.ts("dve", ub[:], G.pI[:], inv_col, ALU.mult, ["pI", "cstf"], [name], s2=sh, op1=ALU.add)
        E.cp("dve", G.ni[:], ub[:], [name], ["ni"])
        E.cp("dve", G.nf[:], G.ni[:], ["ni"], ["nf"])
        E.tt("dve", ub[:], ub[:], G.nf[:], ALU.subtract, [name, "nf"], [name])
        E.stt(ub[:], ub[:], 0.5, ub[:], ALU.is_gt, ALU.subtract, [name], [name])


def rope_tables_act(E, G, c):
    sl = c % 2
    E.act(G.sinT[sl][:], G.uS[:], AF.Sin, ["uS"], [("sinT", sl)], scale=-2.0 * PI * (1.0 - 1e-6))
    E.act(G.cosT[sl][:], G.uC[:], AF.Sin, ["uC"], [("cosT", sl)], scale=-2.0 * PI * (1.0 - 1e-6))


def rope_apply(E, G, raw, rawkey, perm, permkey, nrow, dests, sl):
    b = next_ps(G)
    rawkeys = list(rawkey) if isinstance(rawkey, list) else [rawkey]
    E.mm(G.ps[b][0:nrow, :], perm, raw[0:nrow, :], True, True, rawkeys + [permkey], [("ps", b)])
    s = G.t1c % 2
    G.t1c += 1
    t1, t2 = G.t1[s], G.t2[s]
    E.tt("pool", t1[0:nrow, :], raw[0:nrow, :], G.cosT[sl][0:nrow, :], ALU.mult, rawkeys + [("cosT", sl)], [("t1", s)])
    E.tt("dve", t2[0:nrow, :], G.ps[b][0:nrow, :], G.sinT[sl][0:nrow, :], ALU.mult, [("ps", b), ("sinT", sl)], [("t2", s)])
    for (o, key, r0, r1) in dests:
        E.tt("dve", o, t1[r0:r1, :], t2[r0:r1, :], ALU.add, [("t1", s), ("t2", s)], [key])


def attention(E, G, c, units, scale, vw, pre_unit=None, hooks=None):
    nk = 4 * c + 4
    pairs = [(ui, r) for ui in range(len(units)) for r in range(nk // 2)]
    N = len(pairs)
    pend = []

    def diag_i(kt):
        return kt - 4 * c if kt >= 4 * c else 0

    spread = pre_unit is not None and c >= 1
    blocks = {}

    def qk(n):
        ui, r = pairs[n]
        U = units[ui]
        if r == 0 and pre_unit is not None:
            if spread:
                blocks[ui] = pre_unit(ui + 1)
            else:
                for blk in pre_unit(ui + 1):
                    blk()
        sbk, s = n % 2, n % 3
        qa = 128 * diag_i(2 * r)
        for t in range(2):
            kt = 2 * r + t
            q0 = 128 * diag_i(kt)
            b = 2 * sbk + t
            kap, kkey = U["KT"](kt)
            qap, qkey = U["QT"](q0)
            if kt < 4 * c:
                E.mm(G.ps[b][:, q0:512], kap, qap, True, True, list(kkey) + [qkey], [("ps", b)])
            else:
                E.mm(G.ps[b][:, q0:q0 + 128], kap, qap[:, 0:128], True, False, list(kkey) + [qkey], [("ps", b)])
                E.mm(G.ps[b][:, q0:q0 + 128], G.identb[:], G.trib[:], False, True, ["identb", "trib"], [("ps", b)])
                if q0 + 128 < 512:
                    E.mm(G.ps[b][:, q0 + 128:512], kap, qap[:, 128:512 - q0], True, True, list(kkey) + [qkey],
                         [("ps", b)])
        E.act(G.ET[s][:, :, qa:512], G.ss[sbk][:, :].rearrange("p (t q) -> p t q", t=2)[:, :, qa:512], AF.Exp,
              [("ps", 2 * sbk), ("ps", 2 * sbk + 1)], [("ET", s)], scale=scale)

    def pv(n):
        ui, r = pairs[n]
        U = units[ui]
        s = n % 3
        for t in range(2):
            kt = 2 * r + t
            i = diag_i(kt)
            vap, vkey = U["V"](kt)
            for j in range(i, 4):
                aap, akey, first = U["acc"](j)
                E.mm(aap, G.ET[s][:, t, 128 * j:128 * j + 128], vap, (kt == 0 and first), (kt == 4 * c + j),
                     [("ET", s), vkey], [akey], skip=True)
        if r == nk // 2 - 1:
            rest = U["epi"]()
            if rest is not None:
                pend.append((n + DEFER, rest))

    if pre_unit is not None:
        for blk in pre_unit(0):
            blk()
    qk(0)
    if N > 1:
        qk(1)
    for n in range(N):
        if n + 2 < N:
            qk(n + 2)
        pv(n)
        if spread:
            ui_, r_ = pairs[n]
            if blocks.get(ui_):
                blocks[ui_].pop(0)()
        if hooks and n in hooks:
            hooks.pop(n)()
        while pend and pend[0][0] <= n:
            pend.pop(0)[1]()
    while pend:
        pend.pop(0)[1]()
    assert all(not v for v in blocks.values())
    if hooks:
        for n in sorted(hooks):
            hooks[n]()


def build_nc(SL, dbg=False):
    NT = SL // 128
    NCH = SL // 512
    nc = bass.Bass("TRN2", target_bir_lowering=False)

    def din(name, shape, dt=F32):
        return nc.dram_tensor(name, shape, dt, kind="ExternalInput").ap()
    x = din("x", [SL, 1024])
    p = din("p", [SL, 256])
    pos = din("pos", [128, SL], I32)
    ng = din("ng", [128, 8])
    w_in = din("w_in", [1024, 3104])
    dl = din("dl", [128, 256])
    gsub = din("gsub", [128, 128])
    qg = din("qg", [128, 3])
    kvg = din("kvg", [128, 1])
    w_uq = din("w_uq", [384, 768])
    w_ukv = din("w_ukv", [128, 1024])
    w_out = din("w_out", [1024, 1024])
    w_ple = din("w_ple", [256, 1024])
    w_pg = din("w_pg", [1024, 1024])
    fg = din("fg", [128, 1024])
    cst = din("cst", [128, 482])
    out = nc.dram_tensor("out", [SL, 1024], F32, kind="ExternalOutput").ap()
    cat = nc.dram_tensor("cat", [SL, 1024], BF16, kind="Internal").ap()

    with contextlib.ExitStack() as gst:
        pool = SemPool(nc, gst, CHANS)
        with nc.Block() as blk0:
            @blk0.gpsimd
            def _(g):
                for sm in pool.all():
                    g.sem_clear(sm)

        with contextlib.ExitStack() as st:
            G = Ctx()
            alloc_common(nc, st, G, "A_")
            sb = G.sb
            S = Sched(pool)
            E = Em(S)
            Wd = sb("Wd", [128, 8, 2048], BF16)
            KT = sb("KTd", [128, 4, SL], BF16)
            V = sb("Vd", [128, NT, 4, 129], BF16)
            G.qraw = [sb("qraw%d" % i, [128, 512], BF16) for i in range(3)]
            QT0 = sb("QT0", [128, 4, 512], BF16)
            QT1 = sb("QT1", [128, 4, 512], BF16)
            G.pI = sb("pI", [128, 512], I32)
            G.ni = sb("ni", [128, 512], I32)
            G.uS = sb("uS", [128, 512], F32)
            G.uC = sb("uC", [128, 512], F32)
            G.nf = sb("nf", [128, 512], F32)
            G.cosT = [sb("cosT%d" % i, [128, 512], F32) for i in range(2)]
            G.sinT = [sb("sinT%d" % i, [128, 512], F32) for i in range(2)]
            G.t1 = [sb("t1_%d" % i, [128, 512], F32) for i in range(2)]
            G.t2 = [sb("t2_%d" % i, [128, 512], F32) for i in range(2)]
            G.t1c = 0
            gate = sb("gate", [128, 4, 512], BF16)
            G.ET = [sb("ET%d" % i, [128, 2, 512], BF16) for i in range(3)]
            tmp1 = [sb("tmp1_%d" % i, [128, 4, 128], F32) for i in range(2)]
            ob = [sb("ob_%d" % i, [128, 4, 128], F32) for i in range(2)]
            odt = [sb("odt%d" % i, [128, 4, 512], BF16) for i in range(2)]
            dlt = sb("dlt", [128, 256], F32)
            prod = sb("prod", [128, 128], F32)
            lamt = sb("lamt", [128, 8], F32)
            gs8 = sb("gs8", [128, 128], F32)
            est = sb("est", [128, 2, 24], F32)
            G.epsb = sb("epsb", [128, 1], F32)

            E.memset("pool", G.mhalf[:], -0.5, ["mhalf"])
            for h in range(4):
                E.memset("dve", QT0[:, h, :], 0.0, [("QT", h)])
                E.memset("dve", QT1[:, h, :], 0.0, [("QT", h)])
            load_consts(E, G, cst)
            E.dma(G.ngt[:], ng[:, :], [], ["gcols"], "misc")
            E.dma(dlt[:], dl[:, :], [], ["dlt"], "misc")
            E.dma(gs8[:], gsub[:, :], [], ["gs8"], "misc")
            E.ts("dve", gs8[:], gs8[:], 0.8, ALU.mult, ["gs8"], ["gs8"])
            E.tt("dve", prod[:, 0:64], dlt[:, 0:64], dlt[:, 64:128], ALU.mult, ["dlt"], ["prod"])
            E.tt("dve", prod[:, 64:128], dlt[:, 128:192], dlt[:, 192:256], ALU.mult, ["dlt", "prod"], ["prod"])
            jb, jk = junk(G)
            E.act(jb[:, 0:64], prod[:, 0:64], AF.Copy, ["prod"], ["lam0", jk], accum=lamt[:, 0:1])
            jb, jk = junk(G)
            E.act(jb[:, 0:64], prod[:, 64:128], AF.Copy, ["prod"], ["lam1", jk], accum=lamt[:, 1:2])
            E.act(lamt[:, 2:4], lamt[:, 0:2], AF.Exp, ["lam0", "lam1"], ["lam2"])
            E.tt("dve", lamt[:, 4:5], lamt[:, 2:3], lamt[:, 3:4], ALU.subtract, ["lam2"], ["lam4"])
            E.ts("dve", lamt[:, 5:6], lamt[:, 4:5], 0.2, ALU.add, ["lam4"], ["nlam"], s2=-1.0, op1=ALU.mult)
            nlam = lamt[:, 5:6]
            E.memset("dve", V[:, :, :, 128:129], 1.0, ["Vall"])
            for j in range(4):
                x_load(E, G, x, j, j)
            rope_tables_dve(E, G, pos, 0, G.cstf[:, 480:481])
            rope_tables_act(E, G, 0)
            if NCH > 1:
                rope_tables_dve(E, G, pos, 1, G.cstf[:, 480:481])
            for j in range(4):
                x_norm(E, G, j, 1.0 / 1024.0)
            for j in range(4):
                x_transposes(E, G, j)
            wcnt = [0]
            for cg in range(4):
                for kc in range(8):
                    load_weight(E, G, Wd[:, kc, cg * 512:(cg + 1) * 512],
                                w_in[kc * 128:(kc + 1) * 128, cg * 512:(cg + 1) * 512], 512,
                                G.ngt[:, kc:kc + 1], ("Wd", kc, cg), wcnt)
            xTk = [("xT", j) for j in range(4)]

            for c in range(NCH):
                sl = c % 2
                if c + 1 < NCH:
                    for j in range(4):
                        x_load(E, G, x, 4 * (c + 1) + j, j)
                pending = None
                for which in range(2):
                    for h in range(4):
                        b = next_ps(G)
                        col0 = which * 512 + h * 128
                        for kc in range(8):
                            E.mm(G.ps[b][:, :], Wd[:, kc, col0:col0 + 128], G.xT[:, kc, :], kc == 0, kc == 7,
                                 [("Wd", kc, which)] + xTk, [("ps", b)])
                        s = G.qc % 3
                        G.qc += 1
                        E.act(G.qraw[s][:], G.ps[b][:, :], AF.Copy, [("ps", b)], [("qraw", s)])
                        if which == 0:
                            dests = [(QT0[0:64, h, :], ("QT", h), 0, 64), (QT1[64:128, h, :], ("QT", h), 64, 128)]
                        else:
                            dests = [(KT[:, h, c * 512:(c + 1) * 512], ("KT", h, c), 0, 128)]
                        if pending is not None:
                            rope_apply(E, G, *pending)
                        pending = (G.qraw[s], ("qraw", s), G.Pdb[:], "Pdb", 128, dests, sl)
                if c + 1 < NCH and c < 3:
                    for j in range(4):
                        x_norm(E, G, j, 1.0 / 1024.0)
                for j in range(4):
                    t = 4 * c + j
                    b = next_ps(G)
                    for kc in range(8):
                        E.mm(G.ps[b][:, :], G.xT[:, kc, j * 128:(j + 1) * 128], Wd[:, kc, 1024:1536], kc == 0, kc == 7,
                             [("Wd", kc, 2), ("xT", j)], [("ps", b)])
                    E.cp("dve", V[:, t, :, 0:128], G.ps[b][:, :].rearrange("p (h e) -> p h e", h=4),
                         [("ps", b), "Vall"], [("V", t)])
                    if j == 0:
                        rope_apply(E, G, *pending)
                    b = next_ps(G)
                    for kc in range(8):
                        E.mm(G.ps[b][:, :], G.xT[:, kc, j * 128:(j + 1) * 128], Wd[:, kc, 1536:2048], kc == 0, kc == 7,
                             [("Wd", kc, 3), ("xT", j)], [("ps", b)])
                    E.act(gate[:, j, :], G.ps[b][:, :], AF.Silu, [("ps", b)], [("gate", j)])
                hooks = None
                if c + 1 < NCH:
                    rope_tables_act(E, G, c + 1)
                    if c == 0:
                        for j in range(4):
                            x_transposes(E, G, j)
                    elif c < 3:
                        hooks = {j: (lambda j=j: x_transposes(E, G, j)) for j in range(4)}
                    else:
                        for j in range(4):
                            x_norm(E, G, j, 1.0 / 1024.0, part="sq")
                        hooks = {j: (lambda j=j: x_norm(E, G, j, 1.0 / 1024.0, part="cast")) for j in range(4)}
                        hooks.update({4 + j: (lambda j=j: x_transposes(E, G, j)) for j in range(4)})
                if c + 2 < NCH:
                    hooks = hooks if hooks is not None else {}
                    hooks[8] = (lambda c=c: rope_tables_dve(E, G, pos, c + 2, G.cstf[:, 480:481]))
                od_c = odt[c % 2]
                units = []
                for h in range(4):
                    for m in range(2):
                        ui = 2 * h + m
                        aset = ui % 2

                        def KTf(kt, h=h, m=m):
                            return KT[:, h, kt * 128:(kt + 1) * 128], [("KT", h, kt // 4)]

                        def QTf(q0, h=h, m=m):
                            return (QT0 if m == 0 else QT1)[:, h, q0:512], ("QT", h)

                        def Vf(kt, h=h):
                            return V[:, kt, h, :], ("V", kt)

                        def accf(j, aset=aset):
                            bank = 4 + 2 * aset + j // 2
                            o = (j % 2) * 129
                            return G.ps[bank][:, o:o + 129], ("ps", bank), (j % 2 == 0)

                        def epi(h=h, m=m, aset=aset, od_c=od_c):
                            hp = h % 2
                            eb = est[:, hp, :]
                            for j in range(4):
                                bank = 4 + 2 * aset + j // 2
                                o = (j % 2) * 129
                                acc = G.ps[bank]
                                rcol = eb[:, 4 * m + j:4 * m + j + 1]
                                E.rcp(rcol, acc[:, o + 128:o + 129], [("ps", bank)], [("r", hp, m, j)])
                                if m == 0:
                                    E.ts("dve", tmp1[hp][:, j, :], acc[:, o:o + 128], rcol, ALU.mult,
                                         [("ps", bank), ("r", hp, m, j)], [("tmp1", hp, j)])
                                else:
                                    rn = eb[:, 8 + j:9 + j]
                                    E.ts("dve", rn, rcol, nlam, ALU.mult, [("r", hp, m, j), "nlam"], [("rn", hp, j)])
                                    E.stt(ob[hp][:, j, :], acc[:, o:o + 128], rn, tmp1[hp][:, j, :], ALU.mult, ALU.add,
                                          [("ps", bank), ("rn", hp, j), ("tmp1", hp, j)], [("ob", hp, j)])
                            if m == 0:
                                return None

                            def rest():
                                for j in range(4):
                                    jb, jk = junk(G)
                                    E.act(jb[:, 0:128], ob[hp][:, j, :], AF.Square, [("ob", hp, j)],
                                          [("ss4", hp, j), jk], accum=eb[:, 12 + j:13 + j])
                                ss4k = [("ss4", hp, j) for j in range(4)]
                                rstd(E, G, eb[:, 20:24], eb[:, 12:16], 1.0 / 128.0, ss4k, [("rs4", hp)])
                                for j in range(4):
                                    E.stt(ob[hp][:, j, :], ob[hp][:, j, :], eb[:, 20 + j:21 + j], gs8[:], ALU.mult, ALU.mult,
                                          [("ob", hp, j), ("rs4", hp), "gs8"], [("ob", hp, j)])
                                E.tt("pool", od_c[:, :, h * 128:(h + 1) * 128], ob[hp][:, :, :],
                                     gate[:, :, h * 128:(h + 1) * 128], ALU.mult,
                                     [("ob", hp, j) for j in range(4)] + [("gate", j) for j in range(4)],
                                     [("odt", c % 2, h)])
                            return rest
                        units.append(dict(KT=KTf, QT=QTf, V=Vf, acc=accf, epi=epi))
                attention(E, G, c, units, 0.125, 128, hooks=hooks)
                E.dma(cat[c * 512:(c + 1) * 512, 0:512].rearrange("(j p) f -> p j f", p=128), od_c[:, :, :],
                      [("odt", c % 2, h) for h in range(4)], [], "cat", q="pool")
            S.emit(nc, final_chans=["cat"])

        with contextlib.ExitStack() as st:
            G = Ctx()
            alloc_common(nc, st, G, "B_")
            sb = G.sb
            S = Sched(pool)
            E = Em(S)
            Wm = sb("Wm", [128, 8, 1056], BF16)
            Wuq = sb("Wuq", [128, 3, 768], BF16)
            Wukv = sb("Wukv", [128, 1024], BF16)
            ckvT = sb("ckvT", [128, SL], BF16)
            Vm = sb("Vm", [128, NT, 8, 65], BF16)
            KTw = [sb("KTw%d" % i, [128, SL], BF16) for i in range(2)]
            G.qraw = [sb("qraw%d" % i, [128, 512], BF16) for i in range(3)]
            QT = sb("QTm", [128, 8, 512], BF16)
            G.pI = sb("pI", [128, 512], I32)
            G.ni = sb("ni", [128, 512], I32)
            G.uS = sb("uS", [128, 512], F32)
            G.uC = sb("uC", [128, 512], F32)
            G.nf = sb("nf", [128, 512], F32)
            G.cosT = [sb("cosT%d" % i, [128, 512], F32) for i in range(2)]
            G.sinT = [sb("sinT%d" % i, [128, 512], F32) for i in range(2)]
            G.t1 = [sb("t1_%d" % i, [128, 512], F32) for i in range(2)]
            G.t2 = [sb("t2_%d" % i, [128, 512], F32) for i in range(2)]
            G.t1c = 0
            gate = sb("gate", [128, 4, 512], BF16)
            G.ET = [sb("ET%d" % i, [128, 2, 512], BF16) for i in range(3)]
            omt = [sb("omt%d" % i, [128, 4, 512], BF16) for i in range(2)]
            cqn = [sb("cqn%d" % i, [128, 512], BF16) for i in range(4)]
            psT2 = [G.ps[6].bitcast(BF16), G.ps[7].bitcast(BF16)]
            G.nrot = 6
            nst = sb("nst", [128, 16], F32)
            cqT = sb("cqT", [128, 4, 512], BF16)
            krpad = [sb("krpad%d" % i, [128, 96], BF16) for i in range(4)]
            krraw = sb("krraw", [128, 512], BF16)
            gcol = sb("gcol", [128, 4], F32)
            est = sb("est", [128, 4, 4], F32)
            G.epsb = sb("epsb", [128, 1], F32)

            E.memset("pool", G.mhalf[:], -0.5, ["mhalf"])
            load_consts(E, G, cst)
            E.dma(G.ngt[:], ng[:, :], [], ["gcols"], "misc")
            E.dma(gcol[:, 0:3], qg[:, :], [], ["gcols"], "misc")
            E.dma(gcol[:, 3:4], kvg[:, :], [], ["gcols"], "misc")
            E.memset("dve", Vm[:, :, :, 64:65], 1.0, ["Vall"])
            for i in range(4):
                E.memset("pool", krpad[i][:], 0.0, ["krpad"])
            E.memset("pool", krraw[:], 0.0, ["krraw_all"])
            for j in range(4):
                x_load(E, G, x, j, j)
            rope_tables_dve(E, G, pos, 0, G.cstf[:, 481:482])
            rope_tables_act(E, G, 0)
            if NCH > 1:
                rope_tables_dve(E, G, pos, 1, G.cstf[:, 481:482])
            for j in range(4):
                x_norm(E, G, j, 1.0 / 1024.0)
            for j in range(4):
                x_transposes(E, G, j)
            wcnt = [0]
            for cg, (c0, c1) in enumerate(((0, 512), (512, 544), (544, 1056))):
                for kc in range(8):
                    load_weight(E, G, Wm[:, kc, c0:c1], w_in[kc * 128:(kc + 1) * 128, 2048 + c0:2048 + c1], c1 - c0,
                                G.ngt[:, kc:kc + 1], ("Wm", kc, cg), wcnt)
            for kc in range(3):
                load_weight(E, G, Wuq[:, kc, :], w_uq[kc * 128:(kc + 1) * 128, :], 768, gcol[:, kc:kc + 1], "Wuq", wcnt)
            load_weight(E, G, Wukv[:, :], w_ukv[:, :], 1024, gcol[:, 3:4], "Wukv", wcnt)
            xTk = [("xT", j) for j in range(4)]

            for c in range(NCH):
                sl = c % 2
                if c + 1 < NCH:
                    for j in range(4):
                        x_load(E, G, x, 4 * (c + 1) + j, j)
                for j in range(4):
                    b = next_ps(G)
                    for kc in range(8):
                        E.mm(G.ps[b][:, :], G.xT[:, kc, j * 128:(j + 1) * 128], Wm[:, kc, 0:512], kc == 0, kc == 7,
                             [("Wm", kc, 0), ("xT", j)], [("ps", b)])
                    sq, sk = nst[:, j:j + 1], nst[:, 4 + j:5 + j]
                    rq, rk = nst[:, 8 + j:9 + j], nst[:, 12 + j:13 + j]
                    jb, jk = junk(G)
                    E.act(jb[:, 0:384], G.ps[b][:, 0:384], AF.Square, [("ps", b)], [("sq", j), jk], accum=sq)
                    jb, jk = junk(G)
                    E.act(jb[:, 0:128], G.ps[b][:, 384:512], AF.Square, [("ps", b)], [("sk", j), jk], accum=sk)
                    rstd(E, G, rq, sq, 1.0 / 384.0, [("sq", j)], [("rq", j)])
                    rstd(E, G, rk, sk, 1.0 / 128.0, [("sk", j)], [("rk", j)])
                    E.ts("dve", cqn[j][:, 0:384], G.ps[b][:, 0:384], rq, ALU.mult, [("ps", b), ("rq", j)], [("cqn", j)])
                    E.ts("dve", cqn[j][:, 384:512], G.ps[b][:, 384:512], rk, ALU.mult, [("ps", b), ("rk", j)], [("cqn", j)])
                for j in range(4):
                    b = next_ps(G)
                    for kc in range(8):
                        E.mm(G.ps[b][:, 0:32], G.xT[:, kc, j * 128:(j + 1) * 128], Wm[:, kc, 512:544], kc == 0, kc == 7,
                             [("Wm", kc, 1), ("xT", j)], [("ps", b)])
                    E.cp("dve", krpad[j][:, 64:96], G.ps[b][:, 0:32], [("ps", b), "krpad"], [("krpad", j)])
                for j in range(4):
                    b = next_ps(G)
                    for kc in range(8):
                        E.mm(G.ps[b][:, :], G.xT[:, kc, j * 128:(j + 1) * 128], Wm[:, kc, 544:1056], kc == 0, kc == 7,
                             [("Wm", kc, 2), ("xT", j)], [("ps", b)])
                    E.act(gate[:, j, :], G.ps[b][:, :], AF.Silu, [("ps", b)], [("gate", j)])
                hooks = None
                if c + 1 < NCH:
                    rope_tables_act(E, G, c + 1)
                    for j in range(4):
                        x_norm(E, G, j, 1.0 / 1024.0, part="sq")
                    hooks = {j: (lambda j=j: x_norm(E, G, j, 1.0 / 1024.0, part="cast")) for j in range(4)}
                    hooks.update({4 + j: (lambda j=j: x_transposes(E, G, j)) for j in range(4)})
                if c + 2 < NCH:
                    hooks[8] = (lambda c=c: rope_tables_dve(E, G, pos, c + 2, G.cstf[:, 481:482]))
                for j in range(4):
                    t = 4 * c + j
                    bk = 6 + j % 2
                    pT_ = psT2[j % 2]
                    for kc in range(4):
                        E.tr(pT_[:, kc * 128:(kc + 1) * 128], cqn[j][:, kc * 128:(kc + 1) * 128], G.identb[:],
                             [("cqn", j), "identb"], [("ps", bk)])
                    E.tr(pT_[0:96, 512:640], krpad[j][:, :], G.identb[:], [("krpad", j), "identb"], [("ps", bk)])
                    E.cp("dve", cqT[:, 0:3, j * 128:(j + 1) * 128],
                         pT_[:, 0:384].rearrange("p (k t) -> p k t", k=3), [("ps", bk)], [("cqT", j)])
                    E.cp("dve", ckvT[:, t * 128:(t + 1) * 128], pT_[:, 384:512], [("ps", bk)], [("ckvT", t)])
                    E.cp("dve", krraw[64:96, j * 128:(j + 1) * 128], pT_[64:96, 512:640], [("ps", bk), "krraw_all"],
                         [("krraw", j)])
                for j in range(4):
                    t = 4 * c + j
                    b = next_ps(G)
                    E.mm(G.ps[b][:, :], ckvT[:, t * 128:(t + 1) * 128], Wukv[:, 512:1024], True, True,
                         [("ckvT", t), "Wukv"], [("ps", b)])
                    E.cp("dve", Vm[:, t, :, 0:64], G.ps[b][:, :].rearrange("p (h e) -> p h e", h=8),
                         [("ps", b), "Vall"], [("V", t)])
                cqTk = [("cqT", j) for j in range(4)]
                pending = (krraw, [("krraw", j) for j in range(4)], G.Pmb[0:96, :], "Pmb", 96,
                           [(KTw[0][64:96, c * 512:(c + 1) * 512], ("KTr", 0, c), 64, 96),
                            (KTw[1][64:96, c * 512:(c + 1) * 512], ("KTr", 1, c), 64, 96)], sl)
                for h in range(8):
                    b = next_ps(G)
                    for kc in range(3):
                        E.mm(G.ps[b][0:96, :], Wuq[:, kc, h * 96:(h + 1) * 96], cqT[:, kc, :], kc == 0, kc == 2,
                             ["Wuq"] + cqTk, [("ps", b)])
                    s = G.qc % 3
                    G.qc += 1
                    E.act(G.qraw[s][0:96, :], G.ps[b][0:96, :], AF.Copy, [("ps", b)], [("qraw", s)])
                    rope_apply(E, G, *pending)
                    pending = (G.qraw[s], ("qraw", s), G.Pmb[0:96, :], "Pmb", 96,
                               [(QT[0:96, h, :], ("QT", h), 0, 96)], sl)
                rope_apply(E, G, *pending)
                om_c = omt[c % 2]

                def pre_unit(ui, c=c):
                    if ui >= 8:
                        return []
                    kb = ui % 2

                    def blk(jj):
                        b = 6 + (G.kc % 2)
                        G.kc += 1
                        E.mm(G.ps[b][:, :], Wukv[:, ui * 64:ui * 64 + 128], ckvT[:, jj * 512:(jj + 1) * 512], True, True,
                             ["Wukv"] + [("ckvT", 4 * jj + q) for q in range(4)], [("ps", b)])
                        E.cp("dve", KTw[kb][0:64, jj * 512:(jj + 1) * 512], G.ps[b][0:64, :], [("ps", b)],
                             [("KTn", kb, jj)])
                    return [(lambda jj=jj: blk(jj)) for jj in range(c + 1)]

                units = []
                for h in range(8):
                    kb = h % 2
                    bank = 4 + h % 2

                    def KTf(kt, kb=kb):
                        return KTw[kb][0:96, kt * 128:(kt + 1) * 128], [("KTn", kb, kt // 4), ("KTr", kb, kt // 4)]

                    def QTf(q0, h=h):
                        return QT[0:96, h, q0:512], ("QT", h)

                    def Vf(kt, h=h):
                        return Vm[:, kt, h, :], ("V", kt)

                    def accf(j, bank=bank):
                        return G.ps[bank][:, j * 65:j * 65 + 65], ("ps", bank), (j == 0)

                    def epi(h=h, bank=bank, om_c=om_c):
                        eb = est[:, h % 2, :]
                        for j in range(4):
                            acc = G.ps[bank]
                            rcol = eb[:, j:j + 1]
                            E.rcp(rcol, acc[:, j * 65 + 64:j * 65 + 65], [("ps", bank)], [("r", h % 2, j)])
                            E.stt(om_c[:, j, h * 64:(h + 1) * 64], acc[:, j * 65:j * 65 + 64], rcol,
                                  gate[:, j, h * 64:(h + 1) * 64], ALU.mult, ALU.mult,
                                  [("ps", bank), ("r", h % 2, j), ("gate", j)], [("omt", c % 2, h, j)])
                    units.append(dict(KT=KTf, QT=QTf, V=Vf, acc=accf, epi=epi))
                for h in range(8):
                    pass
                attention(E, G, c, units, float(96 ** -0.5), 64, pre_unit=pre_unit, hooks=hooks)
                E.dma(cat[c * 512:(c + 1) * 512, 512:1024].rearrange("(j p) f -> p j f", p=128), om_c[:, :, :],
                      [("omt", c % 2, h, j) for h in range(8) for j in range(4)], [], "cat", q="pool")
            S.emit(nc, final_chans=["cat"])

        with contextlib.ExitStack() as st:
            G = Ctx()
            alloc_common(nc, st, G, "C_")
            sb = G.sb
            S = Sched(pool)
            E = Em(S)
            Wout = sb("Wout", [128, 8, 1024], BF16)
            Wpg = sb("Wpg", [128, 8, 1024], BF16)
            Wple = sb("Wple", [128, 2, 1024], BF16)
            fgt = sb("fgt", [128, 1024], F32)
            ct = [sb("ct%d" % i, [128, 1024], BF16) for i in range(3)]
            xf = [sb("xf%d" % i, [128, 1024], F32) for i in range(4)]
            pin = [sb("pin%d" % i, [128, 256], F32) for i in range(3)]
            pb = [sb("pb%d" % i, [128, 256], BF16) for i in range(2)]
            pT = [sb("pT%d" % i, [128, 2, 128], BF16) for i in range(2)]
            catT = [sb("catT%d" % i, [128, 8, 128], BF16) for i in range(2)]
            hb = [sb("hb%d" % i, [128, 1024], BF16) for i in range(2)]
            hT = [sb("hT%d" % i, [128, 8, 128], BF16) for i in range(2)]
            sg = [sb("sg%d" % i, [128, 1024], F32) for i in range(2)]
            G.epsb = sb("epsb", [128, 1], F32)
            G.nrot = 6
            psTs = [(G.ps[6].bitcast(BF16), ("ps", 6)), (G.ps[7].bitcast(BF16), ("ps", 7))]
            tcnt = [0]

            def transposes(src, srckey, n, dst, dstkey, eng="dve"):
                pt, pk = psTs[tcnt[0] % 2]
                tcnt[0] += 1
                for kc in range(n):
                    E.tr(pt[:, kc * 128:(kc + 1) * 128], src[:, kc * 128:(kc + 1) * 128], G.identb[:],
                         [srckey, "identb"], [pk])
                E.cp(eng, dst[:, :, :], pt[:, 0:n * 128].rearrange("p (k t) -> p k t", k=n), [pk], [dstkey])

            E.memset("pool", G.mhalf[:], -0.5, ["mhalf"])
            load_consts(E, G, cst)
            E.dma(fgt[:], fg[:, :], [], ["fgt"], "misc")
            def loads(t):
                s3 = t % 3
                E.dma(ct[s3][:], cat[t * 128:(t + 1) * 128, :], [], [("ct", s3)], "ct%d" % s3)
                E.dma(xf[t % 4][:], x[t * 128:(t + 1) * 128, :], [], [("xf", t % 4)], "xf%d" % (t % 4))
                E.dma(pin[s3][:], p[t * 128:(t + 1) * 128, :], [], [("pin", s3)], "p%d" % s3)

            def T1(t):
                s, s3 = t % 2, t % 3
                transposes(ct[s3], ("ct", s3), 8, catT[s], ("catT", s))

            def M1(t):
                s, s4 = t % 2, t % 4
                for half in range(2):
                    hs = slice(half * 512, (half + 1) * 512)
                    b = next_ps(G)
                    for kc in range(8):
                        E.mm(G.ps[b][:, :], catT[s][:, kc, :], Wout[:, kc, hs], kc == 0, kc == 7,
                             [("catT", s), ("Wout", kc, half)], [("ps", b)])
                    E.tt("dve", xf[s4][:, hs], xf[s4][:, hs], G.ps[b][:, :], ALU.add, [("ps", b), ("xf", s4)],
                         [("xf", s4)])
                E.act(hb[s][:], xf[s4][:], AF.Copy, [("xf", s4)], [("hb", s)])
                E.cp("pool", pb[s][:], pin[t % 3][:], [("pin", t % 3)], [("pb", s)])

            def T23(t):
                s = t % 2
                transposes(hb[s], ("hb", s), 8, hT[s], ("hT", s))
                transposes(pb[s], ("pb", s), 2, pT[s], ("pT", s))

            def M23(t):
                s, s4 = t % 2, t % 4
                for half in range(2):
                    hs = slice(half * 512, (half + 1) * 512)
                    b = next_ps(G)
                    for kc in range(8):
                        E.mm(G.ps[b][:, :], hT[s][:, kc, :], Wpg[:, kc, hs], kc == 0, kc == 7,
                             [("hT", s), ("Wpg", kc, half)], [("ps", b)])
                    E.act(sg[s][:, hs], G.ps[b][:, :], AF.Sigmoid, [("ps", b)], [("sg", s, half)])
                    b = next_ps(G)
                    for kc in range(2):
                        E.mm(G.ps[b][:, :], pT[s][:, kc, :], Wple[:, kc, hs], kc == 0, kc == 1,
                             [("pT", s), ("Wple", kc, half)], [("ps", b)])
                    E.tt("dve", sg[s][:, hs], G.ps[b][:, :], sg[s][:, hs], ALU.mult, [("ps", b), ("sg", s, half)],
                         [("sg", s, half)])
                    E.tt("pool", xf[s4][:, hs], xf[s4][:, hs], sg[s][:, hs], ALU.add, [("sg", s, half), ("xf", s4)],
                         [("xf", s4)])

            def tail(t):
                s, s4 = t % 2, t % 4
                ssq = G.stat[:, s:s + 1]
                rs = G.stat[:, 2 + s:3 + s]
                jb, jk = junk(G)
                E.act(jb[:], xf[s4][:], AF.Square, [("xf", s4)], [("ssq", s), jk], accum=ssq)
                rstd(E, G, rs, ssq, 1.0 / 1024.0, [("ssq", s)], [("rs", s)])
                E.stt(xf[s4][:], xf[s4][:], rs, fgt[:], ALU.mult, ALU.mult, [("xf", s4), ("rs", s), "fgt"], [("xf", s4)])
                E.dma(out[t * 128:(t + 1) * 128, :], xf[s4][:], [("xf", s4)], [], "st%d" % s4, q="pool")

            loads(0)
            if NT > 1:
                loads(1)
            T1(0)
            wcnt = [0]
            for Wt, src, nk_, nm in ((Wout, w_out, 8, "Wout"), (Wpg, w_pg, 8, "Wpg"), (Wple, w_ple, 2, "Wple")):
                for half in range(2):
                    for kc in range(nk_):
                        load_weight(E, G, Wt[:, kc, half * 512:(half + 1) * 512],
                                    src[kc * 128:(kc + 1) * 128, half * 512:(half + 1) * 512], 512, None,
                                    (nm, kc, half), wcnt)

            M1(0)
            for t in range(NT):
                if t + 2 < NT:
                    loads(t + 2)
                if t + 1 < NT:
                    T1(t + 1)
                T23(t)
                if t >= 1:
                    tail(t - 1)
                if t + 1 < NT:
                    M1(t + 1)
                M23(t)
            tail(NT - 1)
            S.emit(nc, final_chans=["st0", "st1", "st2", "st3"])
    return nc


def make_consts():
    cstv = np.zeros((128, 482), np.float32)
    cstv[:, 0:128] = np.eye(128, dtype=np.float32)
    kk = np.arange(128)[:, None]
    qq = np.arange(128)[None, :]
    cstv[:, 128:256] = (kk <= qq).astype(np.float32)
    Pd = np.zeros((128, 128), np.float32)
    for m in range(128):
        if m % 64 < 32:
            Pd[m + 32, m] = -1.0
        else:
            Pd[m - 32, m] = 1.0
    cstv[:, 256:384] = Pd
    Pm = np.zeros((128, 96), np.float32)
    for m in range(64, 96):
        if (m - 64) < 16:
            Pm[m + 16, m] = -1.0
        else:
            Pm[m - 16, m] = 1.0
    cstv[:, 384:480] = Pm
    i32 = np.arange(128) % 32
    cstv[:, 480] = (10000.0 ** (-(2.0 * i32) / 64.0)) / (2.0 * np.pi)
    invm = np.zeros(128, np.float64)
    i16 = np.arange(32) % 16
    invm[64:96] = (10000.0 ** (-(2.0 * i16) / 32.0)) / (2.0 * np.pi)
    cstv[:, 481] = invm
    return cstv


_NC_CACHE = {}


def kernel(x, p, positions, norm_g, w_in, diff_lambda, diff_subln_g, mla_q_norm_g, w_uq,
           mla_kv_norm_g, w_ukv, w_out, w_ple, w_ple_gate, final_norm_g):
    x = np.asarray(x)
    B, SL, _ = x.shape
    f = lambda a: np.ascontiguousarray(np.asarray(a, dtype=np.float32))
    rep = lambda v, n=128: np.ascontiguousarray(np.broadcast_to(np.asarray(v, np.float32).reshape(1, -1), (n, np.asarray(v).size)))
    wukv = np.asarray(w_ukv[0], np.float32).reshape(128, 8, 2, 64)
    wukv_l = np.ascontiguousarray(np.concatenate([wukv[:, :, 0, :].reshape(128, 512), wukv[:, :, 1, :].reshape(128, 512)], axis=1))
    shared = {
        "ng": f(np.asarray(norm_g[0]).reshape(8, 128).T),
        "w_in": f(w_in[0]),
        "dl": rep(np.asarray(diff_lambda[0]).reshape(-1)),
        "gsub": rep(diff_subln_g[0]),
        "qg": f(np.asarray(mla_q_norm_g[0]).reshape(3, 128).T),
        "kvg": f(np.asarray(mla_kv_norm_g[0]).reshape(1, 128).T),
        "w_uq": f(w_uq[0]),
        "w_ukv": wukv_l,
        "w_out": f(w_out[0]),
        "w_ple": f(w_ple[0]),
        "w_pg": f(w_ple_gate[0]),
        "fg": rep(final_norm_g),
        "cst": make_consts(),
    }
    positions = np.asarray(positions)
    p = np.asarray(p)
    in_maps = []
    for b in range(B):
        m = dict(shared)
        m["x"] = f(x[b])
        m["p"] = f(p[0, b])
        m["pos"] = np.ascontiguousarray(np.broadcast_to(positions[b].astype(np.int32)[None, :], (128, SL)))
        in_maps.append(m)
    if SL not in _NC_CACHE:
        _NC_CACHE[SL] = build_nc(SL)
    nc = _NC_CACHE[SL]
    res = run_bass_kernel_spmd(nc, in_maps, core_ids=list(range(B)))
    return np.stack([np.asarray(r["out"], dtype=np.float32) for r in res.results], axis=0)
```

```python
import contextlib
import numpy as np
import concourse.bass as bass
import concourse.mybir as mybir
from concourse.bass_utils import run_bass_kernel_spmd

F32 = mybir.dt.float32
BF16 = mybir.dt.bfloat16
I32 = mybir.dt.int32
AF = mybir.ActivationFunctionType
ALU = mybir.AluOpType
PI = float(np.pi)
EPS = 1e-6
ENGS = ("pe", "act", "dve", "pool", "sp")
SAME_ENGINE_SYNC = True
DEFER = 2


class Op:
    __slots__ = ("eng", "fn", "waits", "flag", "eidx", "chan", "count", "done_vc", "fcount")


class SemPool:
    def __init__(self, nc, stack, chans):
        self.sems = {e: stack.enter_context(nc.semaphore("sem_" + e)) for e in ENGS}
        self.csems = {c: stack.enter_context(nc.semaphore("c_" + c)) for c in chans}
        self.base = {e: 0 for e in ENGS}
        self.cbase = {c: 0 for c in chans}

    def all(self):
        return list(self.sems.values()) + list(self.csems.values())


class Sched:
    def __init__(self, pool):
        self.pool = pool
        self.streams = {e: [] for e in ENGS}
        self.last_w = {}
        self.readers = {}
        self.known = {e: {} for e in ENGS}
        self.chan_count = dict(pool.cbase)
        self.last_dma = {}

    def add(self, eng, fn, reads=(), writes=(), chan=None):
        op = Op()
        op.eng, op.fn, op.chan, op.flag = eng, fn, chan, False
        op.eidx = len(self.streams[eng])
        deps = []
        for b in reads:
            w = self.last_w.get(b)
            if w is not None:
                deps.append(w)
        for b in writes:
            w = self.last_w.get(b)
            if w is not None:
                deps.append(w)
            deps.extend(self.readers.get(b, ()))
        known = self.known[eng]
        waits = {}
        for d in deps:
            if d.chan is not None:
                key, val = ("chan", d.chan), self.chan_count[d.chan]
            else:
                key, val = d.eng, d.eidx
                if d.eng == eng and (eng == "pe" or not SAME_ENGINE_SYNC):
                    continue
            if known.get(key, -1) >= val:
                continue
            if key not in waits or waits[key][0] < val:
                waits[key] = (val, d)
        if chan is not None:
            prev = self.last_dma.get(chan)
            key = ("chan", chan)
            if prev is not None and known.get(key, -1) < self.chan_count[chan] and key not in waits:
                waits[key] = (self.chan_count[chan], prev)
        op.waits = [(d, v) for (v, d) in waits.values()]
        for d, _v in op.waits:
            if d.chan is None:
                d.flag = True
            for k, v in d.done_vc.items():
                if known.get(k, -1) < v:
                    known[k] = v
            if d.chan is not None:
                known[("chan", d.chan)] = _v
        if chan is not None:
            self.chan_count[chan] += 16
            op.count = self.chan_count[chan]
            op.done_vc = dict(known)
            op.done_vc[("chan", chan)] = op.count
            self.last_dma[chan] = op
        else:
            op.done_vc = dict(known)
            op.done_vc[eng] = op.eidx
        self.streams[eng].append(op)
        for b in writes:
            self.last_w[b] = op
            self.readers[b] = []
        for b in reads:
            if b not in writes:
                self.readers.setdefault(b, []).append(op)
        return op

    def emit(self, nc, final_chans=()):
        pool = self.pool
        for e in ENGS:
            c = pool.base[e]
            for op in self.streams[e]:
                if op.chan is None and op.flag:
                    c += 1
                op.fcount = c
            pool.base[e] = c
        with nc.Block() as block:
            decos = {"pe": block.tensor, "act": block.scalar, "dve": block.vector,
                     "pool": block.gpsimd, "sp": block.sync}
            for e in ENGS:
                def body(eo, e=e):
                    for op in self.streams[e]:
                        for d, v in op.waits:
                            if d.chan is not None:
                                eo.wait_ge(pool.csems[d.chan], v)
                            else:
                                eo.wait_ge(pool.sems[d.eng], d.fcount)
                        inst = op.fn(eo)
                        if op.chan is not None:
                            inst.then_inc(pool.csems[op.chan], 16)
                        elif op.flag:
                            inst.then_inc(pool.sems[e], 1)
                    if e == "sp":
                        for c in final_chans:
                            eo.wait_ge(pool.csems[c], self.chan_count[c])
                decos[e](body)
        pool.cbase = dict(self.chan_count)


CHANS = ["cst", "w0", "w1", "w2", "w3", "x0", "x1", "x2", "x3", "pos", "misc", "st0", "st1", "st2", "st3", "cat", "p0", "p1", "ct0", "ct1", "xf0", "xf1", "ct2", "xf2", "p2", "xf3"]


class Em:
    def __init__(self, S):
        self.S = S

    def mm(self, out, lhsT, rhs, start, stop, r, w, skip=False):
        self.S.add("pe", lambda e: e.matmul(out, lhsT=lhsT, rhs=rhs, start=start, stop=stop,
                                            skip_group_check=skip), r, w)

    def tr(self, out, in_, ident, r, w):
        self.S.add("pe", lambda e: e.transpose(out=out, in_=in_, identity=ident), r, w)

    def act(self, out, in_, func, r, w, scale=None, bias=None, accum=None):
        kw = {}
        if scale is not None:
            kw["scale"] = scale
        if bias is not None:
            kw["bias"] = bias
        if accum is not None:
            kw["accum_out"] = accum
        self.S.add("act", lambda e: e.activation(out=out, in_=in_, func=func, **kw), r, w)

    def ts(self, eng, out, in0, s1, op0, r, w, s2=None, op1=None):
        if op1 is None:
            self.S.add(eng, lambda e: e.tensor_scalar(out=out, in0=in0, scalar1=s1, scalar2=None, op0=op0), r, w)
        else:
            self.S.add(eng, lambda e: e.tensor_scalar(out=out, in0=in0, scalar1=s1, scalar2=s2, op0=op0, op1=op1), r, w)

    def tt(self, eng, out, in0, in1, op, r, w):
        self.S.add(eng, lambda e: e.tensor_tensor(out=out, in0=in0, in1=in1, op=op), r, w)

    def stt(self, out, in0, scalar, in1, op0, op1, r, w, accum=None):
        if accum is None:
            self.S.add("dve", lambda e: e.scalar_tensor_tensor(out=out, in0=in0, scalar=scalar, in1=in1,
                                                               op0=op0, op1=op1), r, w)
        else:
            self.S.add("dve", lambda e: e.scalar_tensor_tensor(out=out, in0=in0, scalar=scalar, in1=in1,
                                                               op0=op0, op1=op1, accum_out=accum), r, w)

    def cp(self, eng, out, in_, r, w):
        self.S.add(eng, lambda e: e.tensor_copy(out=out, in_=in_), r, w)

    def rcp(self, out, in_, r, w):
        self.S.add("dve", lambda e: e.reciprocal(out=out, in_=in_), r, w)

    def memset(self, eng, ap, val, w):
        self.S.add(eng, lambda e: e.memset(ap, val), (), w)

    def dma(self, out, in_, r, w, chan, q="sp"):
        self.S.add(q, lambda e: e.dma_start(out=out, in_=in_), r, w, chan=chan)


class Ctx:
    pass


def alloc_common(nc, st, G, pfx):
    def sb(name, shape, dt):
        return st.enter_context(nc.sbuf_tensor(pfx + name, shape, dt))
    G.sb = sb
    G.cstf = sb("cstf", [128, 482], F32)
    G.identb = sb("identb", [128, 128], BF16)
    G.trib = sb("trib", [128, 128], BF16)
    G.Pdb = sb("Pdb", [128, 128], BF16)
    G.Pmb = sb("Pmb", [128, 96], BF16)
    G.ngt = sb("ngt", [128, 8], F32)
    G.stage = [sb("stage%d" % i, [128, 512], F32) for i in range(4)]
    G.xin = [sb("xin%d" % i, [128, 1024], F32) for i in range(4)]
    G.xn = [sb("xn%d" % i, [128, 1024], BF16) for i in range(4)]
    G.xT = sb("xT", [128, 8, 512], BF16)
    G.stat = sb("stat", [128, 16], F32)
    G.ss = [st.enter_context(nc.psum_tensor(pfx + "ss%d" % i, [128, 1024], F32)) for i in range(2)]
    G.ps = [G.ss[0][:, 0:512], G.ss[0][:, 512:1024], G.ss[1][:, 0:512], G.ss[1][:, 512:1024]]
    G.ps += [st.enter_context(nc.psum_tensor(pfx + "ps%d" % i, [128, 512], F32))[:, :] for i in range(4, 8)]
    G.psT = G.ps[7].bitcast(BF16)
    G.junkb = [sb("junkb%d" % i, [128, 1024], BF16) for i in range(1)]
    G.mhalf = sb("mhalf", [128, 4], F32)
    G.jc = 0
    G.rr = 0
    G.nrot = 7
    G.qc = 0
    G.kc = 0


def load_consts(E, G, cst):
    E.dma(G.cstf[:], cst[:, :], [], ["cstf"], "cst")
    E.cp("dve", G.identb[:], G.cstf[:, 0:128], ["cstf"], ["identb"])
    E.ts("dve", G.trib[:], G.cstf[:, 128:256], -1.0, ALU.add, ["cstf"], ["trib"], s2=30000.0, op1=ALU.mult)
    E.cp("dve", G.Pdb[:], G.cstf[:, 256:384], ["cstf"], ["Pdb"])
    E.cp("dve", G.Pmb[:], G.cstf[:, 384:480], ["cstf"], ["Pmb"])


def load_weight(E, G, dst, src, ncols, gcol, key, cnt):
    c0 = 0
    while c0 < ncols:
        n = min(512, ncols - c0)
        s = cnt[0] % 4
        cnt[0] += 1
        E.dma(G.stage[s][:, 0:n], src[:, c0:c0 + n], [], [("stage", s)], "w%d" % s)
        if s % 2 == 0:
            if gcol is None:
                E.cp("dve", dst[:, c0:c0 + n], G.stage[s][:, 0:n], [("stage", s)], [key])
            else:
                E.ts("dve", dst[:, c0:c0 + n], G.stage[s][:, 0:n], gcol, ALU.mult, [("stage", s), "gcols"], [key])
        else:
            if gcol is None:
                E.act(dst[:, c0:c0 + n], G.stage[s][:, 0:n], AF.Copy, [("stage", s)], [key])
            else:
                E.act(dst[:, c0:c0 + n], G.stage[s][:, 0:n], AF.Copy, [("stage", s), "gcols"], [key], scale=gcol)
        c0 += n


def rstd(E, G, out, in_, scale, r, w):
    n = out.shape[1]
    E.ts("pool", out, in_, scale, ALU.mult, r, w, s2=EPS, op1=ALU.add)
    E.tt("pool", out, out, G.mhalf[:, 0:n], ALU.pow, list(w) + ["mhalf"], w)


def junk(G):
    i = 0
    return G.junkb[i], ("junk", i)


def next_ps(G):
    b = G.rr % G.nrot
    G.rr += 1
    return b


def x_load(E, G, x, t, j):
    E.dma(G.xin[j][:], x[t * 128:(t + 1) * 128, :], [], [("xin", j)], "x%d" % j)


def x_norm(E, G, j, rst_scale, part="all"):
    ssq = G.stat[:, j:j + 1]
    rs = G.stat[:, 4 + j:5 + j]
    if part in ("all", "sq"):
        jb, jk = junk(G)
        E.act(jb[:], G.xin[j][:], AF.Square, [("xin", j)], [("ssq", j), jk], accum=ssq)
    if part in ("all", "cast"):
        rstd(E, G, rs, ssq, rst_scale, [("ssq", j)], [("rs", j)])
        E.ts("dve", G.xn[j][:], G.xin[j][:], rs, ALU.mult, [("xin", j), ("rs", j)], [("xn", j)])


def x_transposes(E, G, j):
    for kc in range(8):
        E.tr(G.psT[:, kc * 128:(kc + 1) * 128], G.xn[j][:, kc * 128:(kc + 1) * 128], G.identb[:],
             [("xn", j), "identb"], [("ps", 7)])
    E.cp("dve", G.xT[:, :, j * 128:(j + 1) * 128],
         G.psT[:, :].rearrange("p (k t) -> p k t", k=8), [("ps", 7)], [("xT", j)])


def rope_tables_dve(E, G, pos, c, inv_col):
    E.dma(G.pI[:], pos[:, c * 512:(c + 1) * 512], [], ["pI"], "pos")
    for name, ub, sh in (("uS", G.uS, 0.0), ("uC", G.uC, 0.25)):
        E.ts("dve", ub[:], G.pI[:], inv_col, ALU.mult, ["pI", "cstf"], [name], s2=sh, op1=ALU.add)
        E.cp("dve", G.ni[:], ub[:], [name], ["ni"])
        E.cp("dve", G.nf[:], G.ni[:], ["ni"], ["nf"])
        E.tt("dve", ub[:], ub[:], G.nf[:], ALU.subtract, [name, "nf"], [name])
        E.stt(ub[:], ub[:], 0.5, ub[:], ALU.is_gt, ALU.subtract, [name], [name])


def rope_tables_act(E, G, c):
    sl = c % 2
    E.act(G.sinT[sl][:], G.uS[:], AF.Sin, ["uS"], [("sinT", sl)], scale=-2.0 * PI * (1.0 - 1e-6))
    E.act(G.cosT[sl][:], G.uC[:], AF.Sin, ["uC"], [("cosT", sl)], scale=-2.0 * PI * (1.0 - 1e-6))


def rope_apply(E, G, raw, rawkey, perm, permkey, nrow, dests, sl):
    b = next_ps(G)
    rawkeys = list(rawkey) if isinstance(rawkey, list) else [rawkey]
    E.mm(G.ps[b][0:nrow, :], perm, raw[0:nrow, :], True, True, rawkeys + [permkey], [("ps", b)])
    s = G.t1c % 2
    G.t1c += 1
    t1, t2 = G.t1[s], G.t2[s]
    E.tt("pool", t1[0:nrow, :], raw[0:nrow, :], G.cosT[sl][0:nrow, :], ALU.mult, rawkeys + [("cosT", sl)], [("t1", s)])
    E.tt("dve", t2[0:nrow, :], G.ps[b][0:nrow, :], G.sinT[sl][0:nrow, :], ALU.mult, [("ps", b), ("sinT", sl)], [("t2", s)])
    for (o, key, r0, r1) in dests:
        E.tt("dve", o, t1[r0:r1, :], t2[r0:r1, :], ALU.add, [("t1", s), ("t2", s)], [key])


def attention(E, G, c, units, scale, vw, pre_unit=None, hooks=None):
    nk = 4 * c + 4
    pairs = [(ui, r) for ui in range(len(units)) for r in range(nk // 2)]
    N = len(pairs)
    pend = []

    def diag_i(kt):
        return kt - 4 * c if kt >= 4 * c else 0

    spread = pre_unit is not None and c >= 1
    blocks = {}

    def qk(n):
        ui, r = pairs[n]
        U = units[ui]
        if r == 0 and pre_unit is not None:
            if spread:
                blocks[ui] = pre_unit(ui + 1)
            else:
                for blk in pre_unit(ui + 1):
                    blk()
        sbk, s = n % 2, n % 3
        qa = 128 * diag_i(2 * r)
        for t in range(2):
            kt = 2 * r + t
            q0 = 128 * diag_i(kt)
            b = 2 * sbk + t
            kap, kkey = U["KT"](kt)
            qap, qkey = U["QT"](q0)
            if kt < 4 * c:
                E.mm(G.ps[b][:, q0:512], kap, qap, True, True, list(kkey) + [qkey], [("ps", b)])
            else:
                E.mm(G.ps[b][:, q0:q0 + 128], kap, qap[:, 0:128], True, False, list(kkey) + [qkey], [("ps", b)])
                E.mm(G.ps[b][:, q0:q0 + 128], G.identb[:], G.trib[:], False, True, ["identb", "trib"], [("ps", b)])
                if q0 + 128 < 512:
                    E.mm(G.ps[b][:, q0 + 128:512], kap, qap[:, 128:512 - q0], True, True, list(kkey) + [qkey],
                         [("ps", b)])
        E.act(G.ET[s][:, :, qa:512], G.ss[sbk][:, :].rearrange("p (t q) -> p t q", t=2)[:, :, qa:512], AF.Exp,
              [("ps", 2 * sbk), ("ps", 2 * sbk + 1)], [("ET", s)], scale=scale)

    def pv(n):
        ui, r = pairs[n]
        U = units[ui]
        s = n % 3
        for t in range(2):
            kt = 2 * r + t
            i = diag_i(kt)
            vap, vkey = U["V"](kt)
            for j in range(i, 4):
                aap, akey, first = U["acc"](j)
                E.mm(aap, G.ET[s][:, t, 128 * j:128 * j + 128], vap, (kt == 0 and first), (kt == 4 * c + j),
                     [("ET", s), vkey], [akey], skip=True)
        if r == nk // 2 - 1:
            rest = U["epi"]()
            if rest is not None:
                pend.append((n + DEFER, rest))

    if pre_unit is not None:
        for blk in pre_unit(0):
            blk()
    qk(0)
    if N > 1:
        qk(1)
    for n in range(N):
        if n + 2 < N:
            qk(n + 2)
        pv(n)
        if spread:
            ui_, r_ = pairs[n]
            if blocks.get(ui_):
                blocks[ui_].pop(0)()
        if hooks and n in hooks:
            hooks.pop(n)()
        while pend and pend[0][0] <= n:
            pend.pop(0)[1]()
    while pend:
        pend.pop(0)[1]()
    assert all(not v for v in blocks.values())
    if hooks:
        for n in sorted(hooks):
            hooks[n]()


def build_nc(SL, dbg=False):
    NT = SL // 128
    NCH = SL // 512
    nc = bass.Bass("TRN2", target_bir_lowering=False)

    def din(name, shape, dt=F32):
        return nc.dram_tensor(name, shape, dt, kind="ExternalInput").ap()
    x = din("x", [SL, 1024])
    p = din("p", [SL, 256])
    pos = din("pos", [128, SL], I32)
    ng = din("ng", [128, 8])
    w_in = din("w_in", [1024, 3104])
    dl = din("dl", [128, 256])
    gsub = din("gsub", [128, 128])
    qg = din("qg", [128, 3])
    kvg = din("kvg", [128, 1])
    w_uq = din("w_uq", [384, 768])
    w_ukv = din("w_ukv", [128, 1024])
    w_out = din("w_out", [1024, 1024])
    w_ple = din("w_ple", [256, 1024])
    w_pg = din("w_pg", [1024, 1024])
    fg = din("fg", [128, 1024])
    cst = din("cst", [128, 482])
    out = nc.dram_tensor("out", [SL, 1024], F32, kind="ExternalOutput").ap()
    cat = nc.dram_tensor("cat", [SL, 1024], BF16, kind="Internal").ap()

    with contextlib.ExitStack() as gst:
        pool = SemPool(nc, gst, CHANS)
        with nc.Block() as blk0:
            @blk0.gpsimd
            def _(g):
                for sm in pool.all():
                    g.sem_clear(sm)

        with contextlib.ExitStack() as st:
            G = Ctx()
            alloc_common(nc, st, G, "A_")
            sb = G.sb
            S = Sched(pool)
            E = Em(S)
            Wd = sb("Wd", [128, 8, 2048], BF16)
            KT = sb("KTd", [128, 4, SL], BF16)
            V = sb("Vd", [128, NT, 4, 129], BF16)
            G.qraw = [sb("qraw%d" % i, [128, 512], BF16) for i in range(3)]
            QT0 = sb("QT0", [128, 4, 512], BF16)
            QT1 = sb("QT1", [128, 4, 512], BF16)
            G.pI = sb("pI", [128, 512], I32)
            G.ni = sb("ni", [128, 512], I32)
            G.uS = sb("uS", [128, 512], F32)
            G.uC = sb("uC", [128, 512], F32)
            G.nf = sb("nf", [128, 512], F32)
            G.cosT = [sb("cosT%d" % i, [128, 512], F32) for i in range(2)]
            G.sinT = [sb("sinT%d" % i, [128, 512], F32) for i in range(2)]
            G.t1 = [sb("t1_%d" % i, [128, 512], F32) for i in range(2)]
            G.t2 = [sb("t2_%d" % i, [128, 512], F32) for i in range(2)]
            G.t1c = 0
            gate = sb("gate", [128, 4, 512], BF16)
            G.ET = [sb("ET%d" % i, [128, 2, 512], BF16) for i in range(3)]
            tmp1 = [sb("tmp1_%d" % i, [128, 4, 128], F32) for i in range(2)]
            ob = [sb("ob_%d" % i, [128, 4, 128], F32) for i in range(2)]
            odt = [sb("odt%d" % i, [128, 4, 512], BF16) for i in range(2)]
            dlt = sb("dlt", [128, 256], F32)
            prod = sb("prod", [128, 128], F32)
            lamt = sb("lamt", [128, 8], F32)
            gs8 = sb("gs8", [128, 128], F32)
            est = sb("est", [128, 2, 24], F32)
            G.epsb = sb("epsb", [128, 1], F32)

            E.memset("pool", G.mhalf[:], -0.5, ["mhalf"])
            for h in range(4):
                E.memset("dve", QT0[:, h, :], 0.0, [("QT", h)])
                E.memset("dve", QT1[:, h, :], 0.0, [("QT", h)])
            load_consts(E, G, cst)
            E.dma(G.ngt[:], ng[:, :], [], ["gcols"], "misc")
            E.dma(dlt[:], dl[:, :], [], ["dlt"], "misc")
            E.dma(gs8[:], gsub[:, :], [], ["gs8"], "misc")
            E.ts("dve", gs8[:], gs8[:], 0.8, ALU.mult, ["gs8"], ["gs8"])
            E.tt("dve", prod[:, 0:64], dlt[:, 0:64], dlt[:, 64:128], ALU.mult, ["dlt"], ["prod"])
            E.tt("dve", prod[:, 64:128], dlt[:, 128:192], dlt[:, 192:256], ALU.mult, ["dlt", "prod"], ["prod"])
            jb, jk = junk(G)
            E.act(jb[:, 0:64], prod[:, 0:64], AF.Copy, ["prod"], ["lam0", jk], accum=lamt[:, 0:1])
            jb, jk = junk(G)
            E.act(jb[:, 0:64], prod[:, 64:128], AF.Copy, ["prod"], ["lam1", jk], accum=lamt[:, 1:2])
            E.act(lamt[:, 2:4], lamt[:, 0:2], AF.Exp, ["lam0", "lam1"], ["lam2"])
            E.tt("dve", lamt[:, 4:5], lamt[:, 2:3], lamt[:, 3:4], ALU.subtract, ["lam2"], ["lam4"])
            E.ts("dve", lamt[:, 5:6], lamt[:, 4:5], 0.2, ALU.add, ["lam4"], ["nlam"], s2=-1.0, op1=ALU.mult)
            nlam = lamt[:, 5:6]
            E.memset("dve", V[:, :, :, 128:129], 1.0, ["Vall"])
            for j in range(4):
                x_load(E, G, x, j, j)
            rope_tables_dve(E, G, pos, 0, G.cstf[:, 480:481])
            rope_tables_act(E, G, 0)
            if NCH > 1:
                rope_tables_dve(E, G, pos, 1, G.cstf[:, 480:481])
            for j in range(4):
                x_norm(E, G, j, 1.0 / 1024.0)
            for j in range(4):
                x_transposes(E, G, j)
            wcnt = [0]
            for cg in range(4):
                for kc in range(8):
                    load_weight(E, G, Wd[:, kc, cg * 512:(cg + 1) * 512],
                                w_in[kc * 128:(kc + 1) * 128, cg * 512:(cg + 1) * 512], 512,
                                G.ngt[:, kc:kc + 1], ("Wd", kc, cg), wcnt)
            xTk = [("xT", j) for j in range(4)]

            for c in range(NCH):
                sl = c % 2
                if c + 1 < NCH:
                    for j in range(4):
                        x_load(E, G, x, 4 * (c + 1) + j, j)
                pending = None
                for which in range(2):
                    for h in range(4):
                        b = next_ps(G)
                        col0 = which * 512 + h * 128
                        for kc in range(8):
                            E.mm(G.ps[b][:, :], Wd[:, kc, col0:col0 + 128], G.xT[:, kc, :], kc == 0, kc == 7,
                                 [("Wd", kc, which)] + xTk, [("ps", b)])
                        s = G.qc % 3
                        G.qc += 1
                        E.act(G.qraw[s][:], G.ps[b][:, :], AF.Copy, [("ps", b)], [("qraw", s)])
                        if which == 0:
                            dests = [(QT0[0:64, h, :], ("QT", h), 0, 64), (QT1[64:128, h, :], ("QT", h), 64, 128)]
                        else:
                            dests = [(KT[:, h, c * 512:(c + 1) * 512], ("KT", h, c), 0, 128)]
                        if pending is not None:
                            rope_apply(E, G, *pending)
                        pending = (G.qraw[s], ("qraw", s), G.Pdb[:], "Pdb", 128, dests, sl)
                if c + 1 < NCH and c < 3:
                    for j in range(4):
                        x_norm(E, G, j, 1.0 / 1024.0)
                for j in range(4):
                    t = 4 * c + j
                    b = next_ps(G)
                    for kc in range(8):
                        E.mm(G.ps[b][:, :], G.xT[:, kc, j * 128:(j + 1) * 128], Wd[:, kc, 1024:1536], kc == 0, kc == 7,
                             [("Wd", kc, 2), ("xT", j)], [("ps", b)])
                    E.cp("dve", V[:, t, :, 0:128], G.ps[b][:, :].rearrange("p (h e) -> p h e", h=4),
                         [("ps", b), "Vall"], [("V", t)])
                    if j == 0:
                        rope_apply(E, G, *pending)
                    b = next_ps(G)
                    for kc in range(8):
                        E.mm(G.ps[b][:, :], G.xT[:, kc, j * 128:(j + 1) * 128], Wd[:, kc, 1536:2048], kc == 0, kc == 7,
                             [("Wd", kc, 3), ("xT", j)], [("ps", b)])
                    E.act(gate[:, j, :], G.ps[b][:, :], AF.Silu, [("ps", b)], [("gate", j)])
                hooks = None
                if c + 1 < NCH:
                    rope_tables_act(E, G, c + 1)
                    if c == 0:
                        for j in range(4):
                            x_transposes(E, G, j)
                    elif c < 3:
                        hooks = {j: (lambda j=j: x_transposes(E, G, j)) for j in range(4)}
                    else:
                        for j in range(4):
                            x_norm(E, G, j, 1.0 / 1024.0, part="sq")
                        hooks = {j: (lambda j=j: x_norm(E, G, j, 1.0 / 1024.0, part="cast")) for j in range(4)}
                        hooks.update({4 + j: (lambda j=j: x_transposes(E, G, j)) for j in range(4)})
                if c + 2 < NCH:
                    hooks = hooks if hooks is not None else {}
                    hooks[8] = (lambda c=c: rope_tables_dve(E, G, pos, c + 2, G.cstf[:, 480:481]))
                od_c = odt[c % 2]
                units = []
                for h in range(4):
                    for m in range(2):
                        ui = 2 * h + m
                        aset = ui % 2

                        def KTf(kt, h=h, m=m):
                            return KT[:, h, kt * 128:(kt + 1) * 128], [("KT", h, kt // 4)]

                        def QTf(q0, h=h, m=m):
                            return (QT0 if m == 0 else QT1)[:, h, q0:512], ("QT", h)

                        def Vf(kt, h=h):
                            return V[:, kt, h, :], ("V", kt)

                        def accf(j, aset=aset):
                            bank = 4 + 2 * aset + j // 2
                            o = (j % 2) * 129
                            return G.ps[bank][:, o:o + 129], ("ps", bank), (j % 2 == 0)

                        def epi(h=h, m=m, aset=aset, od_c=od_c):
                            hp = h % 2
                            eb = est[:, hp, :]
                            for j in range(4):
                                bank = 4 + 2 * aset + j // 2
                                o = (j % 2) * 129
                                acc = G.ps[bank]
                                rcol = eb[:, 4 * m + j:4 * m + j + 1]
                                E.rcp(rcol, acc[:, o + 128:o + 129], [("ps", bank)], [("r", hp, m, j)])
                                if m == 0:
                                    E.ts("dve", tmp1[hp][:, j, :], acc[:, o:o + 128], rcol, ALU.mult,
                                         [("ps", bank), ("r", hp, m, j)], [("tmp1", hp, j)])
                                else:
                                    rn = eb[:, 8 + j:9 + j]
                                    E.ts("dve", rn, rcol, nlam, ALU.mult, [("r", hp, m, j), "nlam"], [("rn", hp, j)])
                                    E.stt(ob[hp][:, j, :], acc[:, o:o + 128], rn, tmp1[hp][:, j, :], ALU.mult, ALU.add,
                                          [("ps", bank), ("rn", hp, j), ("tmp1", hp, j)], [("ob", hp, j)])
                            if m == 0:
                                return None

                            def rest():
                                for j in range(4):
                                    E.stt(tmp1[hp][:, j, :], ob[hp][:, j, :], 1.0, ob[hp][:, j, :], ALU.mult, ALU.mult,
                                          [("ob", hp, j)], [("ss4", hp, j), ("tmp1", hp, j)], accum=eb[:, 12 + j:13 + j])
                                ss4k = [("ss4", hp, j) for j in range(4)]
                                rstd(E, G, eb[:, 20:24], eb[:, 12:16], 1.0 / 128.0, ss4k, [("rs4", hp)])
                                for j in range(4):
                                    E.stt(ob[hp][:, j, :], ob[hp][:, j, :], eb[:, 20 + j:21 + j], gs8[:], ALU.mult, ALU.mult,
                                          [("ob", hp, j), ("rs4", hp), "gs8"], [("ob", hp, j)])
                                E.tt("pool", od_c[:, :, h * 128:(h + 1) * 128], ob[hp][:, :, :],
                                     gate[:, :, h * 128:(h + 1) * 128], ALU.mult,
                                     [("ob", hp, j) for j in range(4)] + [("gate", j) for j in range(4)],
                                     [("odt", c % 2, h)])
                            return rest
                        units.append(dict(KT=KTf, QT=QTf, V=Vf, acc=accf, epi=epi))
                attention(E, G, c, units, 0.125, 128, hooks=hooks)
                E.dma(cat[c * 512:(c + 1) * 512, 0:512].rearrange("(j p) f -> p j f", p=128), od_c[:, :, :],
                      [("odt", c % 2, h) for h in range(4)], [], "cat", q="pool")
            S.emit(nc, final_chans=["cat"])

        with contextlib.ExitStack() as st:
            G = Ctx()
            alloc_common(nc, st, G, "B_")
            sb = G.sb
            S = Sched(pool)
            E = Em(S)
            Wm = sb("Wm", [128, 8, 1056], BF16)
            Wuq = sb("Wuq", [128, 3, 768], BF16)
            Wukv = sb("Wukv", [128, 1024], BF16)
            ckvT = sb("ckvT", [128, SL], BF16)
            Vm = sb("Vm", [128, NT, 8, 65], BF16)
            KTw = [sb("KTw%d" % i, [128, SL], BF16) for i in range(2)]
            G.qraw = [sb("qraw%d" % i, [128, 512], BF16) for i in range(3)]
            QT = sb("QTm", [128, 8, 512], BF16)
            G.pI = sb("pI", [128, 512], I32)
            G.ni = sb("ni", [128, 512], I32)
            G.uS = sb("uS", [128, 512], F32)
            G.uC = sb("uC", [128, 512], F32)
            G.nf = sb("nf", [128, 512], F32)
            G.cosT = [sb("cosT%d" % i, [128, 512], F32) for i in range(2)]
            G.sinT = [sb("sinT%d" % i, [128, 512], F32) for i in range(2)]
            G.t1 = [sb("t1_%d" % i, [128, 512], F32) for i in range(2)]
            G.t2 = [sb("t2_%d" % i, [128, 512], F32) for i in range(2)]
            G.t1c = 0
            gate = sb("gate", [128, 4, 512], BF16)
            G.ET = [sb("ET%d" % i, [128, 2, 512], BF16) for i in range(3)]
            omt = [sb("omt%d" % i, [128, 4, 512], BF16) for i in range(2)]
            cqn = [sb("cqn%d" % i, [128, 512], BF16) for i in range(4)]
            psT2 = [G.ps[6].bitcast(BF16), G.ps[7].bitcast(BF16)]
            G.nrot = 6
            nst = sb("nst", [128, 16], F32)
            cqT = sb("cqT", [128, 4, 512], BF16)
            krpad = [sb("krpad%d" % i, [128, 96], BF16) for i in range(4)]
            krraw = sb("krraw", [128, 512], BF16)
            gcol = sb("gcol", [128, 4], F32)
            est = sb("est", [128, 4, 4], F32)
            G.epsb = sb("epsb", [128, 1], F32)

            E.memset("pool", G.mhalf[:], -0.5, ["mhalf"])
            load_consts(E, G, cst)
            E.dma(G.ngt[:], ng[:, :], [], ["gcols"], "misc")
            E.dma(gcol[:, 0:3], qg[:, :], [], ["gcols"], "misc")
            E.dma(gcol[:, 3:4], kvg[:, :], [], ["gcols"], "misc")
            E.memset("dve", Vm[:, :, :, 64:65], 1.0, ["Vall"])
            for i in range(4):
                E.memset("pool", krpad[i][:], 0.0, ["krpad"])
            E.memset("pool", krraw[:], 0.0, ["krraw_all"])
            for j in range(4):
                x_load(E, G, x, j, j)
            rope_tables_dve(E, G, pos, 0, G.cstf[:, 481:482])
            rope_tables_act(E, G, 0)
            if NCH > 1:
                rope_tables_dve(E, G, pos, 1, G.cstf[:, 481:482])
            for j in range(4):
                x_norm(E, G, j, 1.0 / 1024.0)
            for j in range(4):
                x_transposes(E, G, j)
            wcnt = [0]
            for cg, (c0, c1) in enumerate(((0, 512), (512, 544), (544, 1056))):
                for kc in range(8):
                    load_weight(E, G, Wm[:, kc, c0:c1], w_in[kc * 128:(kc + 1) * 128, 2048 + c0:2048 + c1], c1 - c0,
                                G.ngt[:, kc:kc + 1], ("Wm", kc, cg), wcnt)
            for kc in range(3):
                load_weight(E, G, Wuq[:, kc, :], w_uq[kc * 128:(kc + 1) * 128, :], 768, gcol[:, kc:kc + 1], "Wuq", wcnt)
            load_weight(E, G, Wukv[:, :], w_ukv[:, :], 1024, gcol[:, 3:4], "Wukv", wcnt)
            xTk = [("xT", j) for j in range(4)]

            for c in range(NCH):
                sl = c % 2
                if c + 1 < NCH:
                    for j in range(4):
                        x_load(E, G, x, 4 * (c + 1) + j, j)
                for j in range(4):
                    b = next_ps(G)
                    for kc in range(8):
                        E.mm(G.ps[b][:, :], G.xT[:, kc, j * 128:(j + 1) * 128], Wm[:, kc, 0:512], kc == 0, kc == 7,
                             [("Wm", kc, 0), ("xT", j)], [("ps", b)])
                    sq, sk = nst[:, j:j + 1], nst[:, 4 + j:5 + j]
                    rq, rk = nst[:, 8 + j:9 + j], nst[:, 12 + j:13 + j]
                    jb, jk = junk(G)
                    E.act(jb[:, 0:384], G.ps[b][:, 0:384], AF.Square, [("ps", b)], [("sq", j), jk], accum=sq)
                    jb, jk = junk(G)
                    E.act(jb[:, 0:128], G.ps[b][:, 384:512], AF.Square, [("ps", b)], [("sk", j), jk], accum=sk)
                    rstd(E, G, rq, sq, 1.0 / 384.0, [("sq", j)], [("rq", j)])
                    rstd(E, G, rk, sk, 1.0 / 128.0, [("sk", j)], [("rk", j)])
                    E.ts("dve", cqn[j][:, 0:384], G.ps[b][:, 0:384], rq, ALU.mult, [("ps", b), ("rq", j)], [("cqn", j)])
                    E.ts("dve", cqn[j][:, 384:512], G.ps[b][:, 384:512], rk, ALU.mult, [("ps", b), ("rk", j)], [("cqn", j)])
                for j in range(4):
                    b = next_ps(G)
                    for kc in range(8):
                        E.mm(G.ps[b][:, 0:32], G.xT[:, kc, j * 128:(j + 1) * 128], Wm[:, kc, 512:544], kc == 0, kc == 7,
                             [("Wm", kc, 1), ("xT", j)], [("ps", b)])
                    E.cp("dve", krpad[j][:, 64:96], G.ps[b][:, 0:32], [("ps", b), "krpad"], [("krpad", j)])
                for j in range(4):
                    b = next_ps(G)
                    for kc in range(8):
                        E.mm(G.ps[b][:, :], G.xT[:, kc, j * 128:(j + 1) * 128], Wm[:, kc, 544:1056], kc == 0, kc == 7,
                             [("Wm", kc, 2), ("xT", j)], [("ps", b)])
                    E.act(gate[:, j, :], G.ps[b][:, :], AF.Silu, [("ps", b)], [("gate", j)])
                hooks = None
                if c + 1 < NCH:
                    rope_tables_act(E, G, c + 1)
                    for j in range(4):
                        x_norm(E, G, j, 1.0 / 1024.0, part="sq")
                    hooks = {j: (lambda j=j: x_norm(E, G, j, 1.0 / 1024.0, part="cast")) for j in range(4)}
                    hooks.update({4 + j: (lambda j=j: x_transposes(E, G, j)) for j in range(4)})
                if c + 2 < NCH:
                    hooks[8] = (lambda c=c: rope_tables_dve(E, G, pos, c + 2, G.cstf[:, 481:482]))
                for j in range(4):
                    t = 4 * c + j
                    bk = 6 + j % 2
                    pT_ = psT2[j % 2]
                    for kc in range(4):
                        E.tr(pT_[:, kc * 128:(kc + 1) * 128], cqn[j][:, kc * 128:(kc + 1) * 128], G.identb[:],
                             [("cqn", j), "identb"], [("ps", bk)])
                    E.tr(pT_[0:96, 512:640], krpad[j][:, :], G.identb[:], [("krpad", j), "identb"], [("ps", bk)])
                    E.cp("dve", cqT[:, 0:3, j * 128:(j + 1) * 128],
                         pT_[:, 0:384].rearrange("p (k t) -> p k t", k=3), [("ps", bk)], [("cqT", j)])
                    E.cp("dve", ckvT[:, t * 128:(t + 1) * 128], pT_[:, 384:512], [("ps", bk)], [("ckvT", t)])
                    E.cp("dve", krraw[64:96, j * 128:(j + 1) * 128], pT_[64:96, 512:640], [("ps", bk), "krraw_all"],
                         [("krraw", j)])
                for j in range(4):
                    t = 4 * c + j
                    b = next_ps(G)
                    E.mm(G.ps[b][:, :], ckvT[:, t * 128:(t + 1) * 128], Wukv[:, 512:1024], True, True,
                         [("ckvT", t), "Wukv"], [("ps", b)])
                    E.cp("dve", Vm[:, t, :, 0:64], G.ps[b][:, :].rearrange("p (h e) -> p h e", h=8),
                         [("ps", b), "Vall"], [("V", t)])
                cqTk = [("cqT", j) for j in range(4)]
                pending = (krraw, [("krraw", j) for j in range(4)], G.Pmb[0:96, :], "Pmb", 96,
                           [(KTw[0][64:96, c * 512:(c + 1) * 512], ("KTr", 0, c), 64, 96),
                            (KTw[1][64:96, c * 512:(c + 1) * 512], ("KTr", 1, c), 64, 96)], sl)
                for h in range(8):
                    b = next_ps(G)
                    for kc in range(3):
                        E.mm(G.ps[b][0:96, :], Wuq[:, kc, h * 96:(h + 1) * 96], cqT[:, kc, :], kc == 0, kc == 2,
                             ["Wuq"] + cqTk, [("ps", b)])
                    s = G.qc % 3
                    G.qc += 1
                    E.act(G.qraw[s][0:96, :], G.ps[b][0:96, :], AF.Copy, [("ps", b)], [("qraw", s)])
                    rope_apply(E, G, *pending)
                    pending = (G.qraw[s], ("qraw", s), G.Pmb[0:96, :], "Pmb", 96,
                               [(QT[0:96, h, :], ("QT", h), 0, 96)], sl)
                rope_apply(E, G, *pending)
                om_c = omt[c % 2]

                def pre_unit(ui, c=c):
                    if ui >= 8:
                        return []
                    kb = ui % 2

                    def blk(jj):
                        b = 6 + (G.kc % 2)
                        G.kc += 1
                        E.mm(G.ps[b][:, :], Wukv[:, ui * 64:ui * 64 + 128], ckvT[:, jj * 512:(jj + 1) * 512], True, True,
                             ["Wukv"] + [("ckvT", 4 * jj + q) for q in range(4)], [("ps", b)])
                        E.cp("dve", KTw[kb][0:64, jj * 512:(jj + 1) * 512], G.ps[b][0:64, :], [("ps", b)],
                             [("KTn", kb, jj)])
                    return [(lambda jj=jj: blk(jj)) for jj in range(c + 1)]

                units = []
                for h in range(8):
                    kb = h % 2
                    bank = 4 + h % 2

                    def KTf(kt, kb=kb):
                        return KTw[kb][0:96, kt * 128:(kt + 1) * 128], [("KTn", kb, kt // 4), ("KTr", kb, kt // 4)]

                    def QTf(q0, h=h):
                        return QT[0:96, h, q0:512], ("QT", h)

                    def Vf(kt, h=h):
                        return Vm[:, kt, h, :], ("V", kt)

                    def accf(j, bank=bank):
                        return G.ps[bank][:, j * 65:j * 65 + 65], ("ps", bank), (j == 0)

                    def epi(h=h, bank=bank, om_c=om_c):
                        eb = est[:, h % 2, :]
                        for j in range(4):
                            acc = G.ps[bank]
                            rcol = eb[:, j:j + 1]
                            E.rcp(rcol, acc[:, j * 65 + 64:j * 65 + 65], [("ps", bank)], [("r", h % 2, j)])
                            E.stt(om_c[:, j, h * 64:(h + 1) * 64], acc[:, j * 65:j * 65 + 64], rcol,
                                  gate[:, j, h * 64:(h + 1) * 64], ALU.mult, ALU.mult,
                                  [("ps", bank), ("r", h % 2, j), ("gate", j)], [("omt", c % 2, h, j)])
                    units.append(dict(KT=KTf, QT=QTf, V=Vf, acc=accf, epi=epi))
                for h in range(8):
                    pass
                attention(E, G, c, units, float(96 ** -0.5), 64, pre_unit=pre_unit, hooks=hooks)
                E.dma(cat[c * 512:(c + 1) * 512, 512:1024].rearrange("(j p) f -> p j f", p=128), om_c[:, :, :],
                      [("omt", c % 2, h, j) for h in range(8) for j in range(4)], [], "cat", q="pool")
            S.emit(nc, final_chans=["cat"])

        with contextlib.ExitStack() as st:
            G = Ctx()
            alloc_common(nc, st, G, "C_")
            sb = G.sb
            S = Sched(pool)
            E = Em(S)
            Wout = sb("Wout", [128, 8, 1024], BF16)
            Wpg = sb("Wpg", [128, 8, 1024], BF16)
            Wple = sb("Wple", [128, 2, 1024], BF16)
            fgt = sb("fgt", [128, 1024], F32)
            ct = [sb("ct%d" % i, [128, 1024], BF16) for i in range(3)]
            xf = [sb("xf%d" % i, [128, 1024], F32) for i in range(4)]
            pin = [sb("pin%d" % i, [128, 256], F32) for i in range(3)]
            pb = [sb("pb%d" % i, [128, 256], BF16) for i in range(2)]
            pT = [sb("pT%d" % i, [128, 2, 128], BF16) for i in range(2)]
            catT = [sb("catT%d" % i, [128, 8, 128], BF16) for i in range(2)]
            hb = [sb("hb%d" % i, [128, 1024], BF16) for i in range(2)]
            hT = [sb("hT%d" % i, [128, 8, 128], BF16) for i in range(2)]
            sg = [sb("sg%d" % i, [128, 1024], F32) for i in range(2)]
            G.epsb = sb("epsb", [128, 1], F32)
            G.nrot = 6
            psTs = [(G.ps[6].bitcast(BF16), ("ps", 6)), (G.ps[7].bitcast(BF16), ("ps", 7))]
            tcnt = [0]

            def transposes(src, srckey, n, dst, dstkey, eng="dve"):
                pt, pk = psTs[tcnt[0] % 2]
                tcnt[0] += 1
                for kc in range(n):
                    E.tr(pt[:, kc * 128:(kc + 1) * 128], src[:, kc * 128:(kc + 1) * 128], G.identb[:],
                         [srckey, "identb"], [pk])
                E.cp(eng, dst[:, :, :], pt[:, 0:n * 128].rearrange("p (k t) -> p k t", k=n), [pk], [dstkey])

            E.memset("pool", G.mhalf[:], -0.5, ["mhalf"])
            load_consts(E, G, cst)
            E.dma(fgt[:], fg[:, :], [], ["fgt"], "misc")
            def loads(t):
                s3 = t % 3
                E.dma(ct[s3][:], cat[t * 128:(t + 1) * 128, :], [], [("ct", s3)], "ct%d" % s3)
                E.dma(xf[t % 4][:], x[t * 128:(t + 1) * 128, :], [], [("xf", t % 4)], "xf%d" % (t % 4))
                E.dma(pin[s3][:], p[t * 128:(t + 1) * 128, :], [], [("pin", s3)], "p%d" % s3)

            def T1(t):
                s, s3 = t % 2, t % 3
                transposes(ct[s3], ("ct", s3), 8, catT[s], ("catT", s))

            def M1(t):
                s, s4 = t % 2, t % 4
                for half in range(2):
                    hs = slice(half * 512, (half + 1) * 512)
                    b = next_ps(G)
                    for kc in range(8):
                        E.mm(G.ps[b][:, :], catT[s][:, kc, :], Wout[:, kc, hs], kc == 0, kc == 7,
                             [("catT", s), ("Wout", kc, half)], [("ps", b)])
                    E.tt("dve", xf[s4][:, hs], xf[s4][:, hs], G.ps[b][:, :], ALU.add, [("ps", b), ("xf", s4)],
                         [("xf", s4)])
                E.act(hb[s][:], xf[s4][:], AF.Copy, [("xf", s4)], [("hb", s)])
                E.cp("pool", pb[s][:], pin[t % 3][:], [("pin", t % 3)], [("pb", s)])

            def T23(t):
                s = t % 2
                transposes(hb[s], ("hb", s), 8, hT[s], ("hT", s))
                transposes(pb[s], ("pb", s), 2, pT[s], ("pT", s))

            def M23(t):
                s, s4 = t % 2, t % 4
                for half in range(2):
                    hs = slice(half * 512, (half + 1) * 512)
                    b = next_ps(G)
                    for kc in range(8):
                        E.mm(G.ps[b][:, :], hT[s][:, kc, :], Wpg[:, kc, hs], kc == 0, kc == 7,
                             [("hT", s), ("Wpg", kc, half)], [("ps", b)])
                    E.act(sg[s][:, hs], G.ps[b][:, :], AF.Sigmoid, [("ps", b)], [("sg", s, half)])
                    b = next_ps(G)
                    for kc in range(2):
                        E.mm(G.ps[b][:, :], pT[s][:, kc, :], Wple[:, kc, hs], kc == 0, kc == 1,
                             [("pT", s), ("Wple", kc, half)], [("ps", b)])
                    E.tt("dve", sg[s][:, hs], G.ps[b][:, :], sg[s][:, hs], ALU.mult, [("ps", b), ("sg", s, half)],
                         [("sg", s, half)])
                    E.tt("pool", xf[s4][:, hs], xf[s4][:, hs], sg[s][:, hs], ALU.add, [("sg", s, half), ("xf", s4)],
                         [("xf", s4)])

            def tail(t):
                s, s4 = t % 2, t % 4
                ssq = G.stat[:, s:s + 1]
                rs = G.stat[:, 2 + s:3 + s]
                jb, jk = junk(G)
                E.act(jb[:], xf[s4][:], AF.Square, [("xf", s4)], [("ssq", s), jk], accum=ssq)
                rstd(E, G, rs, ssq, 1.0 / 1024.0, [("ssq", s)], [("rs", s)])
                E.stt(xf[s4][:], xf[s4][:], rs, fgt[:], ALU.mult, ALU.mult, [("xf", s4), ("rs", s), "fgt"], [("xf", s4)])
                E.dma(out[t * 128:(t + 1) * 128, :], xf[s4][:], [("xf", s4)], [], "st%d" % s4, q="pool")

            loads(0)
            if NT > 1:
                loads(1)
            T1(0)
            wcnt = [0]
            for Wt, src, nk_, nm in ((Wout, w_out, 8, "Wout"), (Wpg, w_pg, 8, "Wpg"), (Wple, w_ple, 2, "Wple")):
                for half in range(2):
                    for kc in range(nk_):
                        load_weight(E, G, Wt[:, kc, half * 512:(half + 1) * 512],
                                    src[kc * 128:(kc + 1) * 128, half * 512:(half + 1) * 512], 512, None,
                                    (nm, kc, half), wcnt)

            M1(0)
            for t in range(NT):
                if t + 2 < NT:
                    loads(t + 2)
                if t + 1 < NT:
                    T1(t + 1)
                T23(t)
                if t >= 1:
                    tail(t - 1)
                if t + 1 < NT:
                    M1(t + 1)
                M23(t)
            tail(NT - 1)
            S.emit(nc, final_chans=["st0", "st1", "st2", "st3"])
    return nc


def make_consts():
    cstv = np.zeros((128, 482), np.float32)
    cstv[:, 0:128] = np.eye(128, dtype=np.float32)
    kk = np.arange(128)[:, None]
    qq = np.arange(128)[None, :]
    cstv[:, 128:256] = (kk <= qq).astype(np.float32)
    Pd = np.zeros((128, 128), np.float32)
    for m in range(128):
        if m % 64 < 32:
            Pd[m + 32, m] = -1.0
        else:
            Pd[m - 32, m] = 1.0
    cstv[:, 256:384] = Pd
    Pm = np.zeros((128, 96), np.float32)
    for m in range(64, 96):
        if (m - 64) < 16:
            Pm[m + 16, m] = -1.0
        else:
            Pm[m - 16, m] = 1.0
    cstv[:, 384:480] = Pm
    i32 = np.arange(128) % 32
    cstv[:, 480] = (10000.0 ** (-(2.0 * i32) / 64.0)) / (2.0 * np.pi)
    invm = np.zeros(128, np.float64)
    i16 = np.arange(32) % 16
    invm[64:96] = (10000.0 ** (-(2.0 * i16) / 32.0)) / (2.0 * np.pi)
    cstv[:, 481] = invm
    return cstv


_NC_CACHE = {}


def kernel(x, p, positions, norm_g, w_in, diff_lambda, diff_subln_g, mla_q_norm_g, w_uq,
           mla_kv_norm_g, w_ukv, w_out, w_ple, w_ple_gate, final_norm_g):
    x = np.asarray(x)
    B, SL, _ = x.shape
    f = lambda a: np.ascontiguousarray(np.asarray(a, dtype=np.float32))
    rep = lambda v, n=128: np.ascontiguousarray(np.broadcast_to(np.asarray(v, np.float32).reshape(1, -1), (n, np.asarray(v).size)))
    wukv = np.asarray(w_ukv[0], np.float32).reshape(128, 8, 2, 64)
    wukv_l = np.ascontiguousarray(np.concatenate([wukv[:, :, 0, :].reshape(128, 512), wukv[:, :, 1, :].reshape(128, 512)], axis=1))
    shared = {
        "ng": f(np.asarray(norm_g[0]).reshape(8, 128).T),
        "w_in": f(w_in[0]),
        "dl": rep(np.asarray(diff_lambda[0]).reshape(-1)),
        "gsub": rep(diff_subln_g[0]),
        "qg": f(np.asarray(mla_q_norm_g[0]).reshape(3, 128).T),
        "kvg": f(np.asarray(mla_kv_norm_g[0]).reshape(1, 128).T),
        "w_uq": f(w_uq[0]),
        "w_ukv": wukv_l,
        "w_out": f(w_out[0]),
        "w_ple": f(w_ple[0]),
        "w_pg": f(w_ple_gate[0]),
        "fg": rep(final_norm_g),
        "cst": make_consts(),
    }
    positions = np.asarray(positions)
    p = np.asarray(p)
    in_maps = []
    for b in range(B):
        m = dict(shared)
        m["x"] = f(x[b])
        m["p"] = f(p[0, b])
        m["pos"] = np.ascontiguousarray(np.broadcast_to(positions[b].astype(np.int32)[None, :], (128, SL)))
        in_maps.append(m)
    if SL not in _NC_CACHE:
        _NC_CACHE[SL] = build_nc(SL)
    nc = _NC_CACHE[SL]
    res = run_bass_kernel_spmd(nc, in_maps, core_ids=list(range(B)))
    return np.stack([np.asarray(r["out"], dtype=np.float32) for r in res.results], axis=0)
```
